# Optimizing a Trainium2 kernel written in Bass

```python
import math
import jax, jax.numpy as jnp
from jax import lax
import numpy as np


D_MODEL = 2048
BATCH = 4
SEQ = 2048
DEPTH = 2
DEC_BATCH = 128
DEC_SEQ = 8
PAST_LEN = 2048
PAGE_SIZE = 128

HEAD_DIM = 128
SB_HEADS = 8
SB_WIDTH = SB_HEADS * HEAD_DIM
SB_BLOCK = 128
SB_BIAS_INIT = -6.0
CONV_WIDTH = D_MODEL // 2
CONV_B_K = 3
LRU_WIDTH = D_MODEL
LRU_BLOCKS = 8
LRU_BLOCK = LRU_WIDTH // LRU_BLOCKS
CONV_C_K = 4
LRU_C = 8.0
N_EVEN = (DEPTH + 1) // 2
N_ODD = DEPTH // 2
ALPHA = (2.0 * DEPTH) ** 0.25
BETA_INIT = (8.0 * DEPTH) ** -0.25
LN_EPS = 1e-5
EVEN_IN = 4 * SB_WIDTH + 4 * CONV_WIDTH
EVEN_SPLITS = (SB_WIDTH, 2 * SB_WIDTH, 3 * SB_WIDTH, 4 * SB_WIDTH,
               4 * SB_WIDTH + CONV_WIDTH, 4 * SB_WIDTH + 2 * CONV_WIDTH, 4 * SB_WIDTH + 3 * CONV_WIDTH)
ODD_IN = 2 * LRU_WIDTH

kernel_name = 'sb_shortconv_rglru_hybrid_step'


def _layernorm(x, g, b):
    xf = x.astype(jnp.float32)
    mu = jnp.mean(xf, axis=-1, keepdims=True)
    var = jnp.mean(jnp.square(xf - mu), axis=-1, keepdims=True)
    return ((xf - mu) * lax.rsqrt(var + LN_EPS) * g + b).astype(x.dtype)


def _modulation(c, w_ada, b_ada):
    m = jax.nn.silu(c) @ w_ada + b_ada
    shift, scale, gate = jnp.split(m, 3, axis=-1)
    return shift[:, None], scale[:, None], gate[:, None]


def _causal_depthwise_conv(x, buf, w):
    K = w.shape[0]
    T = x.shape[1]
    xp = jnp.concatenate([buf.astype(x.dtype), x], axis=1)
    y = w[0] * xp[:, :T]
    for j in range(1, K):
        y = y + w[j] * xp[:, j:j + T]
    return y, xp[:, -(K - 1):]


def _stick_breaking_attention(q, k, v, bias):
    n_q = q.shape[1]
    n_k = k.shape[1]
    offset = n_k - n_q
    qb = math.gcd(SB_BLOCK, n_q)
    scale = 1.0 / math.sqrt(q.shape[-1])
    bias_f = bias.astype(jnp.float32)[None, :, None, None]
    outs = []
    for i in range(n_q // qb):
        end = offset + (i + 1) * qb
        q_i = q[:, i * qb:(i + 1) * qb]
        k_i = k[:, :end]
        v_i = v[:, :end]
        z = jnp.einsum('bqhd,bkhd->bhqk', q_i, k_i).astype(jnp.float32) * scale + bias_f
        q_pos = offset + i * qb + jnp.arange(qb)
        k_pos = jnp.arange(end)
        mask = k_pos[None, :] < q_pos[:, None]
        log_keep = jnp.where(mask, jax.nn.log_sigmoid(-z), 0.0)
        log_after = lax.cumsum(log_keep, axis=3, reverse=True) - log_keep
        w = jnp.where(mask, jnp.exp(jax.nn.log_sigmoid(z) + log_after), 0.0)
        outs.append(jnp.einsum('bhqk,bkhd->bqhd', w.astype(v.dtype), v_i))
    return jnp.concatenate(outs, axis=1)


def _linear_combine(left, right):
    a_l, b_l = left
    a_r, b_r = right
    return a_l * a_r, a_r * b_l + b_r


def _even_layer(x, c, conv_buf, k_past, v_past, w_ada, b_ada, w_in, sb_bias, w_conv, w_out, ln_g, ln_b):
    n_b, T, _ = x.shape
    shift, scale, gate = _modulation(c, w_ada, b_ada)
    u = x * (1.0 + scale) + shift
    proj = u @ w_in
    q, k, v, z_a, b_g, c_g, x_in, z_b = jnp.split(proj, EVEN_SPLITS, axis=-1)
    q = q.reshape(n_b, T, SB_HEADS, HEAD_DIM)
    k = k.reshape(n_b, T, SB_HEADS, HEAD_DIM)
    v = v.reshape(n_b, T, SB_HEADS, HEAD_DIM)
    if k_past is None:
        k_all, v_all = k, v
    else:
        k_all = jnp.concatenate([k_past.astype(k.dtype), k], axis=1)
        v_all = jnp.concatenate([v_past.astype(v.dtype), v], axis=1)
    o_a = _stick_breaking_attention(q, k_all, v_all, sb_bias).reshape(n_b, T, SB_WIDTH) * jax.nn.silu(z_a)
    if conv_buf is None:
        conv_buf = jnp.zeros((n_b, CONV_B_K - 1, CONV_WIDTH), x.dtype)
    conv_out, new_buf = _causal_depthwise_conv(c_g * x_in, conv_buf, w_conv)
    o_b = b_g * conv_out * jax.nn.silu(z_b)
    mix = jnp.concatenate([o_a, o_b], axis=-1) @ w_out
    y = _layernorm(ALPHA * x + (1.0 + gate) * mix, ln_g, ln_b)
    return y, k, v, new_buf


def _odd_layer(x, c, conv_buf, h0, w_ada, b_ada, w_in, w_conv, b_conv, w_gate_a, b_gate_a,
               w_gate_x, b_gate_x, lam, w_out, ln_g, ln_b):
    n_b, T, _ = x.shape
    shift, scale, gate = _modulation(c, w_ada, b_ada)
    u = x * (1.0 + scale) + shift
    x_r, z = jnp.split(u @ w_in, 2, axis=-1)
    if conv_buf is None:
        conv_buf = jnp.zeros((n_b, CONV_C_K - 1, LRU_WIDTH), x.dtype)
    if h0 is None:
        h0 = jnp.zeros((n_b, LRU_WIDTH), x.dtype)
    xc, new_buf = _causal_depthwise_conv(x_r, conv_buf, w_conv)
    xc = xc + b_conv
    xb = xc.reshape(n_b, T, LRU_BLOCKS, LRU_BLOCK)
    g_a = jnp.einsum('btnc,ncd->btnd', xb, w_gate_a).reshape(n_b, T, LRU_WIDTH) + b_gate_a
    g_x = jnp.einsum('btnc,ncd->btnd', xb, w_gate_x).reshape(n_b, T, LRU_WIDTH) + b_gate_x
    log_a = -LRU_C * jax.nn.sigmoid(g_a.astype(jnp.float32)) * jax.nn.softplus(-lam.astype(jnp.float32))
    a = jnp.exp(log_a)
    b_t = jnp.sqrt(-jnp.expm1(2.0 * log_a)) * jax.nn.sigmoid(g_x.astype(jnp.float32)) * xc.astype(jnp.float32)
    b_t = b_t.at[:, 0].add(a[:, 0] * h0.astype(jnp.float32))
    _, h = lax.associative_scan(_linear_combine, (a, b_t), axis=1)
    o = h.astype(x.dtype) * jax.nn.silu(z)
    mix = o @ w_out
    y = _layernorm(ALPHA * x + (1.0 + gate) * mix, ln_g, ln_b)
    return y, new_buf, h[:, -1].astype(x.dtype)


def setup_inputs(seed: int = 0) -> dict:
    key = jax.random.key(seed)
    ks = iter(jax.random.split(key, 40))

    def nrm(shape, s):
        return jax.random.normal(next(ks), shape, jnp.float32) * s

    n_pages = PAST_LEN // PAGE_SIZE
    n_used = DEC_BATCH * n_pages
    n_phys = n_used + max(1, n_used // 4)
    d_inv = D_MODEL ** -0.5
    x_prompt = nrm((BATCH, SEQ, D_MODEL), 1.0)
    x_sample = nrm((DEC_BATCH, DEC_SEQ, D_MODEL), 1.0)
    c_prompt = nrm((BATCH, D_MODEL), 1.0)
    c_sample = nrm((DEC_BATCH, D_MODEL), 1.0)
    cache_k = nrm((N_EVEN, n_phys, PAGE_SIZE, SB_HEADS, HEAD_DIM), 1.0)
    cache_v = nrm((N_EVEN, n_phys, PAGE_SIZE, SB_HEADS, HEAD_DIM), 1.0)
    state_conv_b = nrm((N_EVEN, DEC_BATCH, CONV_B_K - 1, CONV_WIDTH), 1.0)
    state_conv_c = nrm((N_ODD, DEC_BATCH, CONV_C_K - 1, LRU_WIDTH), 1.0)
    state_h = nrm((N_ODD, DEC_BATCH, LRU_WIDTH), 0.5)
    page_table = jax.random.permutation(next(ks), n_phys)[:n_used].reshape(DEC_BATCH, n_pages).astype(jnp.int32)
    we_ada = nrm((N_EVEN, D_MODEL, 3 * D_MODEL), 0.2 * d_inv)
    be_ada = nrm((N_EVEN, 3 * D_MODEL), 0.01)
    we_in = nrm((N_EVEN, D_MODEL, EVEN_IN), d_inv)
    we_sb_bias = SB_BIAS_INIT + nrm((N_EVEN, SB_HEADS), 0.1)
    we_conv = nrm((N_EVEN, CONV_B_K, CONV_WIDTH), CONV_B_K ** -0.5)
    we_out = nrm((N_EVEN, SB_WIDTH + CONV_WIDTH, D_MODEL), BETA_INIT * (SB_WIDTH + CONV_WIDTH) ** -0.5)
    ge_ln = 1.0 + nrm((N_EVEN, D_MODEL), 0.02)
    be_ln = nrm((N_EVEN, D_MODEL), 0.02)
    wo_ada = nrm((N_ODD, D_MODEL, 3 * D_MODEL), 0.2 * d_inv)
    bo_ada = nrm((N_ODD, 3 * D_MODEL), 0.01)
    wo_in = nrm((N_ODD, D_MODEL, ODD_IN), d_inv)
    wo_conv = nrm((N_ODD, CONV_C_K, LRU_WIDTH), CONV_C_K ** -0.5)
    bo_conv = nrm((N_ODD, LRU_WIDTH), 0.01)
    wo_gate_a = nrm((N_ODD, LRU_BLOCKS, LRU_BLOCK, LRU_BLOCK), LRU_BLOCK ** -0.5)
    bo_gate_a = nrm((N_ODD, LRU_WIDTH), 0.01)
    wo_gate_x = nrm((N_ODD, LRU_BLOCKS, LRU_BLOCK, LRU_BLOCK), LRU_BLOCK ** -0.5)
    bo_gate_x = nrm((N_ODD, LRU_WIDTH), 0.01)
    a0 = jax.random.uniform(next(ks), (N_ODD, LRU_WIDTH), jnp.float32, 0.9, 0.999)
    root = a0 ** (1.0 / LRU_C)
    wo_lambda = jnp.log(root) - jnp.log1p(-root)
    wo_out = nrm((N_ODD, LRU_WIDTH, D_MODEL), BETA_INIT * LRU_WIDTH ** -0.5)
    go_ln = 1.0 + nrm((N_ODD, D_MODEL), 0.02)
    bo_ln = nrm((N_ODD, D_MODEL), 0.02)
    return {'x_prompt': x_prompt, 'x_sample': x_sample, 'c_prompt': c_prompt, 'c_sample': c_sample,
            'cache_k': cache_k, 'cache_v': cache_v, 'state_conv_b': state_conv_b,
            'state_conv_c': state_conv_c, 'state_h': state_h, 'page_table': page_table,
            'we_ada': we_ada, 'be_ada': be_ada, 'we_in': we_in, 'we_sb_bias': we_sb_bias,
            'we_conv': we_conv, 'we_out': we_out, 'ge_ln': ge_ln, 'be_ln': be_ln,
            'wo_ada': wo_ada, 'bo_ada': bo_ada, 'wo_in': wo_in, 'wo_conv': wo_conv, 'bo_conv': bo_conv,
            'wo_gate_a': wo_gate_a, 'bo_gate_a': bo_gate_a, 'wo_gate_x': wo_gate_x, 'bo_gate_x': bo_gate_x,
            'wo_lambda': wo_lambda, 'wo_out': wo_out, 'go_ln': go_ln, 'bo_ln': bo_ln}


def reference(x_prompt, x_sample, c_prompt, c_sample, cache_k, cache_v, state_conv_b, state_conv_c,
              state_h, page_table, we_ada, be_ada, we_in, we_sb_bias, we_conv, we_out, ge_ln, be_ln,
              wo_ada, bo_ada, wo_in, wo_conv, bo_conv, wo_gate_a, bo_gate_a, wo_gate_x, bo_gate_x,
              wo_lambda, wo_out, go_ln, bo_ln):
    n_seq = page_table.shape[0]
    yp, ys = x_prompt, x_sample
    kp_l, vp_l, bp_l, ks_l, vs_l, bs_l = [], [], [], [], [], []
    cp_l, hp_l, cs_l, hs_l = [], [], [], []
    for layer in range(DEPTH):
        i = layer // 2
        if layer % 2 == 0:
            ew = (we_ada[i], be_ada[i], we_in[i], we_sb_bias[i], we_conv[i], we_out[i], ge_ln[i], be_ln[i])
            yp, kp, vp, bp = _even_layer(yp, c_prompt, None, None, None, *ew)
            k_past = cache_k[i][page_table].reshape(n_seq, -1, SB_HEADS, HEAD_DIM)
            v_past = cache_v[i][page_table].reshape(n_seq, -1, SB_HEADS, HEAD_DIM)
            ys, ks_, vs_, bs_ = _even_layer(ys, c_sample, state_conv_b[i], k_past, v_past, *ew)
            kp_l.append(kp); vp_l.append(vp); bp_l.append(bp)
            ks_l.append(ks_); vs_l.append(vs_); bs_l.append(bs_)
        else:
            ow = (wo_ada[i], bo_ada[i], wo_in[i], wo_conv[i], bo_conv[i], wo_gate_a[i], bo_gate_a[i],
                  wo_gate_x[i], bo_gate_x[i], wo_lambda[i], wo_out[i], go_ln[i], bo_ln[i])
            yp, cp, hp = _odd_layer(yp, c_prompt, None, None, *ow)
            ys, cs_, hs_ = _odd_layer(ys, c_sample, state_conv_c[i], state_h[i], *ow)
            cp_l.append(cp); hp_l.append(hp); cs_l.append(cs_); hs_l.append(hs_)
    k_prompt = jnp.stack(kp_l)
    v_prompt = jnp.stack(vp_l)
    conv_b_prompt = jnp.stack(bp_l)
    conv_c_prompt = jnp.stack(cp_l)
    h_prompt = jnp.stack(hp_l)
    k_sample = jnp.stack(ks_l)
    v_sample = jnp.stack(vs_l)
    conv_b_sample = jnp.stack(bs_l)
    conv_c_sample = jnp.stack(cs_l)
    h_sample = jnp.stack(hs_l)
    return (yp, ys, k_prompt, v_prompt, conv_b_prompt, conv_c_prompt, h_prompt,
            k_sample, v_sample, conv_b_sample, conv_c_sample, h_sample)
```

```python
import contextlib
import numpy as np
import concourse.bass as bass
import concourse.mybir as mybir
from concourse.bass_utils import run_bass_kernel_spmd

F32 = mybir.dt.float32
BF16 = mybir.dt.bfloat16
I32 = mybir.dt.int32
AF = mybir.ActivationFunctionType
ALU = mybir.AluOpType

SAME_ENGINE_SYNC = True
NCORES = 8
D = 2048
NT = 512
NPASS = 4
ALPHA = float((2.0 * 2) ** 0.25)
QSCALE = float(1.0 / np.sqrt(128.0))
EPS = 1e-5
NPHYS = 2560

VO = {}
_off = 0
for _nm, _n in [("be_ada", 48), ("bo_ada", 48), ("ge_ln", 16), ("be_ln", 16), ("go_ln", 16),
                ("bo_ln", 16), ("we_conv", 24), ("wo_conv", 64), ("bo_conv", 16), ("bga", 16),
                ("bgx", 16), ("lam", 16), ("sbb", 8)]:
    VO[_nm] = _off
    _off += _n
NV = _off


class _Rec:
    def __init__(self):
        self.call = None

    def __getattr__(self, name):
        def f(*a, **k):
            self.call = (name, a, k)
            return None
        return f


class Sched:
    ENG = ("pe", "act", "dve", "pool", "sp")

    def __init__(self, nc):
        self.nc = nc
        self.ops = {e: [] for e in self.ENG}
        self.n = {e: 0 for e in self.ENG}
        self.flag = {e: set() for e in self.ENG}
        self.dcount = {}
        self.lastw = {}
        self.readers = {}
        self.seen = {e: {} for e in self.ENG}
        self.dpool = {"sp": [f"s{i}" for i in range(14)], "pool": [f"g{i}" for i in range(14)]}
        self.dnext = {"sp": 0, "pool": 0}
        self.lastdma = {}

    def _need(self, eng, tok):
        kind, key, idx = tok
        if kind == "e" and key == eng:
            if eng == "pe" or not SAME_ENGINE_SYNC:
                return
        k = (kind, key)
        if self.seen[eng].get(k, 0) >= idx:
            return
        self.seen[eng][k] = idx
        if kind == "e":
            self.flag[key].add(idx)
        self.ops[eng].append(("wait", tok))

    @staticmethod
    def _exp(keys):
        out = []
        for k in keys:
            if isinstance(k, str) and k[0] == "T" and k[1:].isdigit():
                out += [k + "a", k + "b"]
            else:
                out.append(k)
        return out

    def emit(self, eng, fn, reads=(), writes=(), dma=False):
        reads = self._exp(reads)
        writes = self._exp(writes)
        writes = writes + [k for k in reads if isinstance(k, str) and k.startswith("ps") and k not in writes]
        rec = _Rec()
        fn(rec)
        call = rec.call
        fn = lambda eo, call=call: getattr(eo, call[0])(*call[1], **call[2])
        deps = []
        for r in reads:
            if r in self.lastw:
                deps.append(self.lastw[r])
        for w in writes:
            if w in self.lastw:
                deps.append(self.lastw[w])
            deps.extend(self.readers.get(w, {}).values())
        dname = None
        if dma:
            dname = self.dpool[eng][self.dnext[eng] % len(self.dpool[eng])]
            self.dnext[eng] += 1
            if dname in self.lastdma:
                deps.append(self.lastdma[dname])
        for t in deps:
            self._need(eng, t)
        if not dma:
            self.n[eng] += 1
            tok = ("e", eng, self.n[eng])
            self.ops[eng].append(("op", fn, self.n[eng]))
        else:
            self.dcount[dname] = self.dcount.get(dname, 0) + 16
            tok = ("d", dname, self.dcount[dname])
            self.lastdma[dname] = tok
            self.ops[eng].append(("dma", fn, dname))
        for r in reads:
            self.readers.setdefault(r, {})[(tok[0], tok[1])] = tok
        for w in writes:
            self.lastw[w] = tok
            self.readers[w] = {}
        return tok

    def finish(self, eng="sp"):
        for name, val in self.dcount.items():
            self._need(eng, ("d", name, val))
        for e in self.ENG:
            if e != eng and self.n[e] > 0:
                self._need(eng, ("e", e, self.n[e]))

    def replay(self):
        nc = self.nc
        engobj = {"pe": nc.tensor, "act": nc.scalar, "dve": nc.vector, "pool": nc.gpsimd, "sp": nc.sync}
        sem_names = ["e_" + e for e in self.ENG] + ["d_" + d for d in self.dcount]
        with contextlib.ExitStack() as st:
            sems = {nm: st.enter_context(nc.semaphore(nm)) for nm in sem_names}
            pref = {}
            for e in self.ENG:
                arr = np.zeros(self.n[e] + 1, dtype=np.int64)
                for i in self.flag[e]:
                    arr[i] = 1
                pref[e] = np.cumsum(arr)
            with nc.Block() as block:
                def run(e, eo):
                    for op in self.ops[e]:
                        if op[0] == "wait":
                            kind, key, idx = op[1]
                            if kind == "e":
                                eo.wait_ge(sems["e_" + key], int(pref[key][idx]))
                            else:
                                eo.wait_ge(sems["d_" + key], int(idx))
                        elif op[0] == "op":
                            inst = op[1](eo)
                            if op[2] in self.flag[e]:
                                inst.then_inc(sems["e_" + e], 1)
                        else:
                            inst = op[1](eo)
                            inst.then_inc(sems["d_" + op[2]], 16)

                @block.tensor
                def _(t):
                    run("pe", t)

                @block.scalar
                def _(t):
                    run("act", t)

                @block.vector
                def _(t):
                    run("dve", t)

                @block.gpsimd
                def _(t):
                    run("pool", t)

                @block.sync
                def _(t):
                    run("sp", t)


class _Stop(Exception):
    pass


NWB = 3


def build(do_prompt=True, do_sample=True, npass=NPASS, stage=99, wseq=None):
    nc = bass.Bass("TRN2", target_bir_lowering=False)
    S = Sched(nc)

    in_names = []

    def din(name, shape, dt=F32):
        in_names.append(name)
        return nc.dram_tensor(name, list(shape), dt, kind="ExternalInput").ap()

    def dout(name, shape, dt=F32):
        return nc.dram_tensor(name, list(shape), dt, kind="ExternalOutput").ap()

    xpT = din("xpT", [D, 2048])
    xsT = din("xsT", [D, 128])
    cT = din("cT", [D, 17])
    ptab = din("ptab", [1, 256], I32)
    if do_sample:
        cache_k = din("cache_k", [NPHYS * 128, 1024])
        cache_v = din("cache_v", [NPHYS * 128, 1024])
    scbT = din("scbT", [1024, 16, 2])
    sccT = din("sccT", [D, 16, 3])
    shT = din("shT", [D, 16])
    we_ada = din("we_ada", [128, 16, 6144])
    we_in = din("we_in", [128, 16, 8192])
    we_out = din("we_out", [128, 16, 2048])
    wo_ada = din("wo_ada", [128, 16, 6144])
    wo_in = din("wo_in", [128, 16, 4096])
    wo_out = din("wo_out", [128, 16, 2048])
    wga = din("wga", [128, 8, 2, 256])
    wgx = din("wgx", [128, 8, 2, 256])
    wmap = {"we_ada": we_ada, "we_in": we_in, "we_out": we_out, "wo_ada": wo_ada, "wo_in": wo_in, "wo_out": wo_out}
    vecT_d = din("vecT", [128, NV])
    c_tri = din("c_tri", [128, 3, 128])
    c_maskp = din("c_maskp", [128, 4, 512])
    c_masknew = din("c_masknew", [128, 16, 8])
    c_iota = din("c_iota", [128, 1])

    ypT = dout("ypT", [D, 2048])
    ysT = dout("ysT", [D, 128])
    kpT = dout("kpT", [1024, 2048])
    vp = dout("vp", [2048, 1024])
    cbpT = dout("cbpT", [1024, 2])
    ccpT = dout("ccpT", [D, 3])
    hpT = dout("hpT", [D, 1])
    ksT = dout("ksT", [1024, 128])
    vs = dout("vs", [128, 1024])
    cbsT = dout("cbsT", [1024, 16, 2])
    ccsT = dout("ccsT", [D, 16, 3])
    hsT = dout("hsT", [D, 16])

    def fm(ap):
        if len(ap.shape) == 2:
            return ap.rearrange("(c p) t -> p c t", p=128)
        return ap.rearrange("(c p) s t -> p c s t", p=128)

    st = contextlib.ExitStack()
    with st:
        def sb(name, shape, dt=F32):
            return st.enter_context(nc.sbuf_tensor(name, list(shape), dt))

        KT = sb("KT", [128, 8, 2048], BF16)
        V = sb("V", [128, 16, 1024], BF16)
        R = sb("R", [128, 16, 512], F32)
        uT = sb("uT", [128, 16, 512], BF16)
        mixT = sb("mixT", [128, 16, 512], BF16)
        T = sb("T", [128, 12, 512], F32)
        WB = [sb(f"wb{i}", [128, 16, 256], BF16) for i in range(NWB)]
        GW = [sb(f"gw{i}", [128, 2, 256], BF16) for i in range(4)]
        adaT = sb("adaT", [128, 2, 48, 17], F32)
        siluT = sb("siluT", [128, 16, 17], BF16)
        vecT = sb("vecT_sb", [128, NV], F32)
        tri3 = sb("tri3", [128, 3, 128], BF16)
        maskp = sb("maskp", [128, 4, 512], BF16)
        masknew = sb("masknew", [128, 16, 8], F32)
        iota = sb("iota", [128, 1], F32)
        hcar = sb("hcar", [128, 16], F32)
        clam = sb("clam", [128, 16], F32)
        one1 = sb("one1", [128, 1], F32)
        cxe = sb("cxe", [128, 2, 516], F32)
        cxcar = sb("cxcar", [128, 8, 32], F32)
        xre = sb("xre", [128, 2, 516], F32)
        xrcar = sb("xrcar", [128, 16, 48], F32)
        idx = sb("idx", [128, 256], I32)
        PSt = st.enter_context(nc.psum_tensor("PSt", [128, 8, 512], F32))

        tri = tri3[:, 0, :]
        ones = tri3[:, 1, :]
        ident = tri3[:, 2, :]

        wb_i = [0, 0]
        wrec = []

        def load_w(wdram, col0):
            k = wb_i[0]
            wb_i[0] += 1
            if wseq is None:
                wrec.append((wdram.tensor.name, col0))
                i = k % NWB
                S.emit("pool", lambda e: e.dma_start(out=WB[i][:], in_=wdram[:, :, col0:col0 + 256]),
                       writes=[f"wb{i}"], dma=True)
                return WB[i], f"wb{i}"
            assert wseq[k] == (wdram.tensor.name, col0), (k, col0)
            while wb_i[1] < min(len(wseq), k + NWB):
                j = wb_i[1]
                wb_i[1] += 1
                wd, col_j = wmap[wseq[j][0]], wseq[j][1]
                S.emit("pool", lambda e: e.dma_start(out=WB[j % NWB][:], in_=wd[:, :, col_j:col_j + 256]),
                       writes=[f"wb{j % NWB}"], dma=True)
            return WB[k % NWB], f"wb{k % NWB}"

        ps_i = [0]
        ps_banks = [[0, 1, 2, 3, 6, 7]]

        def ps():
            b = ps_banks[0]
            i = b[ps_i[0] % len(b)]
            ps_i[0] += 1
            return PSt[:, i, :], f"ps{i}"

        def vcol(name, j):
            o = VO[name] + j
            return vecT[:, o:o + 1]

        S.emit("sp", lambda e: e.dma_start(out=vecT[:], in_=vecT_d), writes=["vecT"], dma=True)
        S.emit("pool", lambda e: e.dma_start(out=tri3[:], in_=c_tri), writes=["tri3"], dma=True)
        S.emit("pool", lambda e: e.dma_start(out=maskp[:], in_=c_maskp), writes=["maskp"], dma=True)
        S.emit("sp", lambda e: e.dma_start(out=masknew[:], in_=c_masknew), writes=["masknew"], dma=True)
        S.emit("sp", lambda e: e.dma_start(out=iota[:], in_=c_iota), writes=["iota"], dma=True)
        S.emit("dve", lambda e: e.memset(one1[:], 1.0), writes=["one1"])
        lam = vecT[:, VO["lam"]:VO["lam"] + 16]
        S.emit("act", lambda e: e.activation(out=clam[:], in_=lam, func=AF.Exp, scale=-1.0),
               reads=["vecT"], writes=["clam"])
        S.emit("act", lambda e: e.activation(out=clam[:], in_=clam[:], func=AF.Ln, bias=one1[:, 0:1], scale=1.0),
               reads=["clam", "one1"], writes=["clam"])
        S.emit("dve", lambda e: e.tensor_scalar(out=clam[:], in0=clam[:], scalar1=-8.0, scalar2=None, op0=ALU.mult),
               reads=["clam"], writes=["clam"])

        cTs = T[:, 0, 0:16 * 17].rearrange("p (c s) -> p c s", c=16)
        S.emit("sp", lambda e: e.dma_start(out=cTs, in_=fm(cT)), writes=["T0"], dma=True)
        S.emit("act", lambda e: e.activation(out=siluT[:], in_=cTs, func=AF.Silu), reads=["T0"], writes=["siluT"])
        def ada_block(l, wada, bname, blk):
            wbuf, wkey = load_w(wada, blk * 256)
            for m in range(2):
                ch = blk * 2 + m
                pst, pk = ps()
                for kc in range(16):
                    S.emit("pe", lambda e: e.matmul(
                        pst[:, 0:17], wbuf[:, kc, m * 128:(m + 1) * 128], siluT[:, kc, :],
                        start=(kc == 0), stop=(kc == 15)), reads=[wkey, "siluT"], writes=[pk])
                addc = 1.0 if ch >= 16 else 0.0
                S.emit("dve", lambda e: e.tensor_scalar(
                    out=adaT[:, l, ch, :], in0=pst[:, 0:17], scalar1=vcol(bname, ch), scalar2=addc,
                    op0=ALU.add, op1=ALU.add), reads=[pk, "vecT"], writes=[("adaT", l)])
                if ch >= 32:
                    S.emit("dve", lambda e: e.tensor_scalar(out=adaT[:, l, ch, :], in0=adaT[:, l, ch, :], scalar1=1.0 / ALPHA,
                                                            scalar2=None, op0=ALU.mult), reads=[("adaT", l)], writes=[("adaT", l)])

        for blk in range(24):
            ada_block(0, we_ada, "be_ada", blk)
        bg_tasks = [(lambda blk=blk: ada_block(1, wo_ada, "bo_ada", blk)) for blk in range(24)]

        def bg(k=1):
            for _ in range(k):
                if bg_tasks:
                    bg_tasks.pop(0)()

        if not do_prompt:
            bg(24)

        def modulate(l, src_chunk, src_key, c, nseq, t, col0):
            n = nseq * t
            if nseq == 1:
                S.emit("act", lambda e: e.activation(out=uT[:, c, 0:n], in_=src_chunk, func=AF.Identity,
                                                     scale=adaT[:, l, 16 + c, col0:col0 + 1], bias=adaT[:, l, c, col0:col0 + 1]),
                       reads=[src_key, ("adaT", l)], writes=[("uT", c)])
                return
            sc = adaT[:, l, 16 + c, col0:col0 + nseq].unsqueeze(2).to_broadcast([128, nseq, t])
            sh = adaT[:, l, c, col0:col0 + nseq].unsqueeze(2).to_broadcast([128, nseq, t])
            tmp = T[:, 11, 0:n].rearrange("p (s t) -> p s t", s=nseq)
            srcv = src_chunk.rearrange("p (s t) -> p s t", s=nseq)
            S.emit("dve", lambda e: e.tensor_tensor(out=tmp, in0=srcv, in1=sc, op=ALU.mult),
                   reads=[src_key, ("adaT", l)], writes=["T11"])
            S.emit("dve", lambda e: e.tensor_tensor(out=uT[:, c, 0:n].rearrange("p (s t) -> p s t", s=nseq),
                                                    in0=tmp, in1=sh, op=ALU.add),
                   reads=["T11", ("adaT", l)], writes=[("uT", c)])

        def proj_fm(wbuf, wkey, m, n, src=None, skeys=None):
            pst, pk = ps()
            for kc in range(16):
                S.emit("pe", lambda e, kc=kc: e.matmul(pst[:, 0:n], wbuf[:, kc, m * 128:(m + 1) * 128],
                                                       uT[:, kc, 0:n], start=(kc == 0), stop=(kc == 15)),
                       reads=[wkey, ("uT", kc)], writes=[pk])
            return pst, pk

        def outproj_ln(l, wout, gname, bname, nseq, t, col0, resid_loader, ydst, ydst_is_R, next_l):
            n = nseq * t
            ssum = PSt[:, 4, :]
            ssq = PSt[:, 5, :]
            for blk in range(8):
                wbuf, wkey = load_w(wout, blk * 256)
                for m in range(2):
                    oc = blk * 2 + m
                    pst, pk = ps()
                    for kc in range(16):
                        S.emit("pe", lambda e, kc=kc, wbuf=wbuf, m=m, pst=pst: e.matmul(
                            pst[:, 0:n], wbuf[:, kc, m * 128:(m + 1) * 128], mixT[:, kc, 0:n],
                            start=(kc == 0), stop=(kc == 15)), reads=[wkey, ("mixT", kc)], writes=[pk])
                    xres, xkey = resid_loader(oc)
                    if nseq == 1:
                        S.emit("dve", lambda e: e.scalar_tensor_tensor(
                            out=R[:, oc, 0:n], in0=pst[:, 0:n], scalar=adaT[:, l, 32 + oc, col0:col0 + 1], in1=xres,
                            op0=ALU.mult, op1=ALU.add), reads=[pk, ("adaT", l), xkey], writes=[("R", oc)])
                    else:
                        g1 = adaT[:, l, 32 + oc, col0:col0 + nseq].unsqueeze(2).to_broadcast([128, nseq, t])
                        tmp = T[:, 10, 0:n]
                        S.emit("dve", lambda e: e.tensor_tensor(
                            out=tmp.rearrange("p (s t) -> p s t", s=nseq),
                            in0=pst[:, 0:n].rearrange("p (s t) -> p s t", s=nseq), in1=g1, op=ALU.mult),
                            reads=[pk, ("adaT", l)], writes=["T10"])
                        S.emit("dve", lambda e: e.tensor_tensor(out=R[:, oc, 0:n], in0=xres, in1=tmp, op=ALU.add),
                               reads=[xkey, "T10"], writes=[("R", oc)])
                    rb = T[:, 8, 0:n // 2].bitcast(BF16) if False else None
                    rbf = T[:, 8, :].bitcast(BF16)[:, 0:n]
                    rsq = T[:, 9, :].bitcast(BF16)[:, 0:n]
                    S.emit("act", lambda e, oc=oc, rbf=rbf: e.activation(out=rbf, in_=R[:, oc, 0:n], func=AF.Copy),
                           reads=[("R", oc)], writes=["T8"])
                    S.emit("act", lambda e, oc=oc, rsq=rsq: e.activation(out=rsq, in_=R[:, oc, 0:n], func=AF.Square),
                           reads=[("R", oc)], writes=["T9"])
                    S.emit("pe", lambda e, oc=oc, rbf=rbf: e.matmul(ssum[:, 0:n], ones, rbf, start=(oc == 0), stop=(oc == 15)),
                           reads=["T8", "tri3"], writes=["ps4"])
                    S.emit("pe", lambda e, oc=oc, rsq=rsq: e.matmul(ssq[:, 0:n], ones, rsq, start=(oc == 0), stop=(oc == 15)),
                           reads=["T9", "tri3"], writes=["ps5"])
            mean = T[:, 8, 0:n]
            rstd = T[:, 9, 0:n]
            nmr = T[:, 10, 0:n]
            S.emit("dve", lambda e: e.tensor_scalar(out=mean, in0=ssum[:, 0:n], scalar1=1.0 / D, scalar2=None, op0=ALU.mult),
                   reads=["ps4"], writes=["T8"])
            S.emit("dve", lambda e: e.tensor_tensor(out=nmr, in0=mean, in1=mean, op=ALU.mult), reads=["T8"], writes=["T10"])
            S.emit("dve", lambda e: e.scalar_tensor_tensor(out=rstd, in0=ssq[:, 0:n], scalar=1.0 / D, in1=nmr,
                                                           op0=ALU.mult, op1=ALU.subtract),
                   reads=["ps5", "T10"], writes=["T9"])
            S.emit("dve", lambda e: e.tensor_scalar(out=rstd, in0=rstd, scalar1=EPS / (ALPHA * ALPHA), scalar2=1e-30, op0=ALU.add, op1=ALU.max),
                   reads=["T9"], writes=["T9"])
            S.emit("act", lambda e: e.activation(out=rstd, in_=rstd, func=AF.Sqrt), reads=["T9"], writes=["T9"])
            S.emit("dve", lambda e: e.reciprocal(out=rstd, in_=rstd), reads=["T9"], writes=["T9"])
            S.emit("dve", lambda e: e.scalar_tensor_tensor(out=nmr, in0=mean, scalar=-1.0, in1=rstd, op0=ALU.mult, op1=ALU.mult),
                   reads=["T8", "T9"], writes=["T10"])
            for oc in range(16):
                eng = "dve" if oc % 2 == 0 else "pool"
                S.emit(eng, lambda e, oc=oc: e.tensor_tensor(out=R[:, oc, 0:n], in0=R[:, oc, 0:n], in1=rstd, op=ALU.mult),
                       reads=[("R", oc), "T9"], writes=[("R", oc)])
                S.emit(eng, lambda e, oc=oc: e.tensor_tensor(out=R[:, oc, 0:n], in0=R[:, oc, 0:n], in1=nmr, op=ALU.add),
                       reads=[("R", oc), "T10"], writes=[("R", oc)])
                S.emit("act", lambda e, oc=oc: e.activation(out=R[:, oc, 0:n], in_=R[:, oc, 0:n], func=AF.Identity,
                                                            scale=vcol(gname, oc), bias=vcol(bname, oc)),
                       reads=[("R", oc), "vecT"], writes=[("R", oc)])
                if ydst is not None:
                    S.emit("sp", lambda e, oc=oc: e.dma_start(out=ydst(oc), in_=R[:, oc, 0:n]), reads=[("R", oc)], dma=True)
                if next_l is not None:
                    modulate(next_l, R[:, oc, 0:n], ("R", oc), oc, nseq, t, col0)

        def layer1(nseq, t, col0, first_pass, last_pass, ydst, cc_dst, h_dst):
            n = nseq * t
            W = t + 3
            for blk in range(8):
                wxr, kxr = load_w(wo_in, blk * 256)
                ga_w, gx_w = GW[(2 * blk) % 4], GW[(2 * blk + 1) % 4]
                ga_k, gx_k = f"gw{(2 * blk) % 4}", f"gw{(2 * blk + 1) % 4}"
                S.emit("pool", lambda e, blk=blk, ga_w=ga_w: e.dma_start(out=ga_w[:], in_=wga[:, blk, :, :]), writes=[ga_k], dma=True)
                S.emit("pool", lambda e, blk=blk, gx_w=gx_w: e.dma_start(out=gx_w[:], in_=wgx[:, blk, :, :]), writes=[gx_k], dma=True)
                xcb = T[:, 0, :].bitcast(BF16)
                for m in range(2):
                    c = blk * 2 + m
                    pst, pk = proj_fm(wxr, kxr, m, n)
                    xv = xre[:, m, 0:nseq * W].rearrange("p (s w) -> p s w", s=nseq)
                    if first_pass is None:
                        pass
                    S.emit("pool", lambda e, xv=xv, c=c: e.tensor_copy(
                        out=xv[:, :, 0:3], in_=xrcar[:, c, 0:nseq * 3].rearrange("p (s w) -> p s w", s=nseq)),
                        reads=[("xrcar", c)], writes=[("xre", m)])
                    S.emit("act", lambda e, xv=xv, pst=pst: e.activation(
                        out=xv[:, :, 3:W], in_=pst[:, 0:n].rearrange("p (s t) -> p s t", s=nseq), func=AF.Copy),
                        reads=[pk], writes=[("xre", m)])
                    S.emit("pool", lambda e, xv=xv, c=c: e.tensor_copy(
                        out=xrcar[:, c, 0:nseq * 3].rearrange("p (s w) -> p s w", s=nseq), in_=xv[:, :, t:W]),
                        reads=[("xre", m)], writes=[("xrcar", c)])
                    if last_pass:
                        S.emit("sp", lambda e, c=c: e.dma_start(out=cc_dst(c), in_=xrcar[:, c, 0:nseq * 3]),
                               reads=[("xrcar", c)], dma=True)
                    xc = T[:, 1 + m, 0:n]
                    xc3 = xc.rearrange("p (s t) -> p s t", s=nseq)
                    S.emit("dve", lambda e, xv=xv, xc3=xc3, c=c: e.tensor_scalar(
                        out=xc3, in0=xv[:, :, 0:t], scalar1=vcol("wo_conv", 0 * 16 + c), scalar2=vcol("bo_conv", c),
                        op0=ALU.mult, op1=ALU.add), reads=[("xre", m), "vecT"], writes=[f"T{1 + m}"])
                    for j in range(1, 4):
                        S.emit("dve", lambda e, xv=xv, xc3=xc3, c=c, j=j: e.scalar_tensor_tensor(
                            out=xc3, in0=xv[:, :, j:j + t], scalar=vcol("wo_conv", j * 16 + c), in1=xc3,
                            op0=ALU.mult, op1=ALU.add), reads=[("xre", m), "vecT", f"T{1 + m}"], writes=[f"T{1 + m}"])
                    S.emit("act", lambda e, xc=xc, m=m: e.activation(out=xcb[:, m * 512:m * 512 + n], in_=xc, func=AF.Copy),
                           reads=[f"T{1 + m}"], writes=["T0" + "ab"[m]])
                wz, kz = load_w(wo_in, 2048 + blk * 256)
                for m in range(2):
                    pz, pzk = proj_fm(wz, kz, m, n)
                    S.emit("act", lambda e: e.activation(out=T[:, 8 + m, 0:n], in_=pz[:, 0:n], func=AF.Silu), reads=[pzk], writes=[f"T{8 + m}"])
                for m in range(2):
                    c = blk * 2 + m
                    xc = T[:, 1 + m, 0:n]
                    pa, pak = ps()
                    for kc in range(2):
                        S.emit("pe", lambda e, kc=kc, pa=pa, m=m: e.matmul(
                            pa[:, 0:n], ga_w[:, kc, m * 128:(m + 1) * 128], xcb[:, kc * 512:kc * 512 + n],
                            start=(kc == 0), stop=(kc == 1)), reads=[ga_k, "T0" + "ab"[kc]], writes=[pak])
                    px, pxk = ps()
                    for kc in range(2):
                        S.emit("pe", lambda e, kc=kc, px=px, m=m: e.matmul(
                            px[:, 0:n], gx_w[:, kc, m * 128:(m + 1) * 128], xcb[:, kc * 512:kc * 512 + n],
                            start=(kc == 0), stop=(kc == 1)), reads=[gx_k, "T0" + "ab"[kc]], writes=[pxk])
                    a_t = T[:, 3, 0:n]
                    b_t = T[:, 4, 0:n]
                    g_t = T[:, 5, 0:n]
                    h_t = T[:, 6, 0:n]
                    S.emit("act", lambda e, pa=pa, c=c: e.activation(out=a_t, in_=pa[:, 0:n], func=AF.Sigmoid,
                                                                      bias=vcol("bga", c), scale=1.0),
                           reads=[pak, "vecT"], writes=["T3"])
                    S.emit("act", lambda e, c=c: e.activation(out=a_t, in_=a_t, func=AF.Exp, scale=clam[:, c:c + 1]),
                           reads=["T3", "clam"], writes=["T3"])
                    S.emit("act", lambda e, px=px, c=c: e.activation(out=g_t, in_=px[:, 0:n], func=AF.Sigmoid,
                                                                      bias=vcol("bgx", c), scale=1.0),
                           reads=[pxk, "vecT"], writes=["T5"])
                    S.emit("dve", lambda e: e.scalar_tensor_tensor(out=b_t, in0=a_t, scalar=-1.0, in1=a_t, op0=ALU.mult, op1=ALU.mult),
                           reads=["T3"], writes=["T4"])
                    S.emit("dve", lambda e: e.tensor_scalar(out=b_t, in0=b_t, scalar1=1.0, scalar2=1e-30, op0=ALU.add, op1=ALU.max),
                           reads=["T4"], writes=["T4"])
                    S.emit("act", lambda e: e.activation(out=b_t, in_=b_t, func=AF.Sqrt), reads=["T4"], writes=["T4"])
                    S.emit("dve", lambda e: e.tensor_tensor(out=b_t, in0=b_t, in1=g_t, op=ALU.mult), reads=["T4", "T5"], writes=["T4"])
                    S.emit("dve", lambda e, xc=xc: e.tensor_tensor(out=b_t, in0=b_t, in1=xc, op=ALU.mult),
                           reads=["T4", f"T{1 + m}"], writes=["T4"])
                    if nseq > 1:
                        a3 = a_t.rearrange("p (s t) -> p s t", s=nseq)
                        b3 = b_t.rearrange("p (s t) -> p s t", s=nseq)
                        h0 = T[:, 7, 0:nseq]
                        S.emit("sp", lambda e, c=c: e.dma_start(out=h0, in_=fm(shT)[:, c, :]), writes=["T7"], dma=True)
                        S.emit("dve", lambda e, a3=a3, h0=h0: e.tensor_tensor(out=h0, in0=h0, in1=a3[:, :, 0], op=ALU.mult),
                               reads=["T7", "T3"], writes=["T7"])
                        S.emit("dve", lambda e, b3=b3, h0=h0: e.tensor_tensor(out=b3[:, :, 0], in0=b3[:, :, 0], in1=h0, op=ALU.add),
                               reads=["T7", "T4"], writes=["T4"])
                        S.emit("dve", lambda e, a3=a3: e.memset(a3[:, :, 0], 0.0), reads=["T3"], writes=["T3"])
                        init = 0.0
                        ireads = []
                    else:
                        init = 0.0 if first_pass else hcar[:, c:c + 1]
                        ireads = [] if first_pass else [("hcar", c)]
                    S.emit("dve", lambda e, init=init: e.tensor_tensor_scan(out=h_t, data0=a_t, data1=b_t, initial=init,
                                                                            op0=ALU.mult, op1=ALU.add),
                           reads=["T3", "T4"] + ireads, writes=["T6"])
                    if nseq == 1:
                        S.emit("pool", lambda e, c=c: e.tensor_copy(out=hcar[:, c:c + 1], in_=h_t[:, n - 1:n]),
                               reads=["T6"], writes=[("hcar", c)])
                        if last_pass:
                            S.emit("sp", lambda e, c=c: e.dma_start(out=h_dst(c), in_=hcar[:, c:c + 1]), reads=[("hcar", c)], dma=True)
                    else:
                        h3 = h_t.rearrange("p (s t) -> p s t", s=nseq)
                        S.emit("pool", lambda e, h3=h3: e.tensor_copy(out=T[:, 7, 64:64 + nseq], in_=h3[:, :, t - 1]),
                               reads=["T6"], writes=["T7"])
                        S.emit("sp", lambda e, c=c: e.dma_start(out=h_dst(c), in_=T[:, 7, 64:64 + nseq]), reads=["T7"], dma=True)
                    S.emit("dve", lambda e: e.tensor_tensor(out=mixT[:, c, 0:n], in0=h_t, in1=T[:, 8 + m, 0:n], op=ALU.mult),
                           reads=["T6", f"T{8 + m}"], writes=[("mixT", c)])
            outproj_ln(1, wo_out, "go_ln", "bo_ln", nseq, t, col0,
                       lambda oc: (R[:, oc, 0:n], ("R", oc)), ydst, True, None)

        def convb_branch(nseq, t, first_pass, last_pass, cb_dst):
            n = nseq * t
            W = t + 2
            for pr in range(4):
                wcg, kcg = load_w(we_in, 5120 + pr * 256)
                for m in range(2):
                    pst, pk = proj_fm(wcg, kcg, m, n)
                    S.emit("act", lambda e, pst=pst, m=m: e.activation(out=T[:, 1 + m, 0:n], in_=pst[:, 0:n], func=AF.Copy),
                           reads=[pk], writes=[f"T{1 + m}"])
                bg()
                wxi, kxi = load_w(we_in, 6144 + pr * 256)
                for m in range(2):
                    c = pr * 2 + m
                    pst, pk = proj_fm(wxi, kxi, m, n)
                    xv = cxe[:, m, 0:nseq * W].rearrange("p (s w) -> p s w", s=nseq)
                    S.emit("pool", lambda e, xv=xv, c=c: e.tensor_copy(
                        out=xv[:, :, 0:2], in_=cxcar[:, c, 0:nseq * 2].rearrange("p (s w) -> p s w", s=nseq)),
                        reads=[("cxcar", c)], writes=[("cxe", m)])
                    S.emit("dve", lambda e, xv=xv, pst=pst, m=m: e.tensor_tensor(
                        out=xv[:, :, 2:W], in0=pst[:, 0:n].rearrange("p (s t) -> p s t", s=nseq),
                        in1=T[:, 1 + m, 0:n].rearrange("p (s t) -> p s t", s=nseq), op=ALU.mult),
                        reads=[pk, f"T{1 + m}"], writes=[("cxe", m)])
                    S.emit("pool", lambda e, xv=xv, c=c: e.tensor_copy(
                        out=cxcar[:, c, 0:nseq * 2].rearrange("p (s w) -> p s w", s=nseq), in_=xv[:, :, t:W]),
                        reads=[("cxe", m)], writes=[("cxcar", c)])
                    if last_pass:
                        S.emit("sp", lambda e, c=c: e.dma_start(out=cb_dst(c), in_=cxcar[:, c, 0:nseq * 2]),
                               reads=[("cxcar", c)], dma=True)
                    cv = T[:, 3 + m, 0:n]
                    cv3 = cv.rearrange("p (s t) -> p s t", s=nseq)
                    S.emit("dve", lambda e, xv=xv, cv3=cv3, c=c: e.tensor_scalar(
                        out=cv3, in0=xv[:, :, 0:t], scalar1=vcol("we_conv", 0 * 8 + c), scalar2=None, op0=ALU.mult),
                        reads=[("cxe", m), "vecT"], writes=[f"T{3 + m}"])
                    for j in range(1, 3):
                        S.emit("dve", lambda e, xv=xv, cv3=cv3, c=c, j=j: e.scalar_tensor_tensor(
                            out=cv3, in0=xv[:, :, j:j + t], scalar=vcol("we_conv", j * 8 + c), in1=cv3,
                            op0=ALU.mult, op1=ALU.add), reads=[("cxe", m), "vecT", f"T{3 + m}"], writes=[f"T{3 + m}"])
                bg()
                wbg, kbg = load_w(we_in, 4096 + pr * 256)
                for m in range(2):
                    cv = T[:, 3 + m, 0:n]
                    pst, pk = proj_fm(wbg, kbg, m, n)
                    S.emit("dve", lambda e, pst=pst, cv=cv: e.tensor_tensor(out=cv, in0=pst[:, 0:n], in1=cv, op=ALU.mult),
                           reads=[pk, f"T{3 + m}"], writes=[f"T{3 + m}"])
                bg()
                wzb, kzb = load_w(we_in, 7168 + pr * 256)
                for m in range(2):
                    c = pr * 2 + m
                    cv = T[:, 3 + m, 0:n]
                    pst2, pk2 = proj_fm(wzb, kzb, m, n)
                    S.emit("act", lambda e, pst2=pst2, m=m: e.activation(out=T[:, 5 + m, 0:n], in_=pst2[:, 0:n], func=AF.Silu),
                           reads=[pk2], writes=[f"T{5 + m}"])
                    S.emit("dve", lambda e, cv=cv, m=m, c=c: e.tensor_tensor(out=mixT[:, 8 + c, 0:n], in0=cv, in1=T[:, 5 + m, 0:n], op=ALU.mult),
                           reads=[f"T{3 + m}", f"T{5 + m}"], writes=[("mixT", 8 + c)])

        def gate(k):
            if stage < k:
                raise _Stop()

        try:
            if do_prompt:
              gate(1)
              S.emit("dve", lambda e: e.memset(cxcar[:], 0.0), writes=[("cxcar", c) for c in range(8)])
              S.emit("dve", lambda e: e.memset(xrcar[:], 0.0), writes=[("xrcar", c) for c in range(16)])
              for p in range(npass):
                  tok0 = p * NT
                  first, last = (p == 0), (p == npass - 1)
                  for c in range(16):
                      xs_, xk = T[:, c % 4, :], f"T{c % 4}"
                      S.emit("sp", lambda e, c=c, xs_=xs_: e.dma_start(out=xs_, in_=fm(xpT)[:, c, tok0:tok0 + NT]), writes=[xk], dma=True)
                      modulate(0, xs_, xk, c, 1, NT, 0)
                  gate(1.2)
                  ps_banks[0] = [6, 7]
                  nkb = (tok0 + NT) // 128
                  for hp in range(4):
                      wk, kk = load_w(we_in, 1024 + hp * 256)
                      for m in range(2):
                          h = hp * 2 + m
                          pst, pk = proj_fm(wk, kk, m, NT)
                          S.emit("act", lambda e, pst=pst, h=h: e.activation(out=KT[:, h, tok0:tok0 + NT], in_=pst[:, 0:NT], func=AF.Copy),
                                 reads=[pk], writes=[("KT", h)])
                          S.emit("dve", lambda e, pst=pst, m=m: e.tensor_copy(out=T[:, 8 + m, :], in_=pst[:, 0:NT]),
                                 reads=[pk], writes=[f"T{8 + m}"])
                          S.emit("sp", lambda e, h=h, m=m: e.dma_start(out=kpT[h * 128:(h + 1) * 128, tok0:tok0 + NT], in_=T[:, 8 + m, :]),
                                 reads=[f"T{8 + m}"], dma=True)
                      gate(1.4)
                      bg()
                      wv, kv = load_w(we_in, 2048 + hp * 256)
                      for tt in range(4):
                          pst, pk = ps()
                          for kc in range(16):
                              S.emit("pe", lambda e, kc=kc, pst=pst, tt=tt: e.matmul(
                                  pst[:, 0:256], uT[:, kc, tt * 128:(tt + 1) * 128], wv[:, kc, :],
                                  start=(kc == 0), stop=(kc == 15)), reads=[kv, ("uT", kc)], writes=[pk])
                          kb = tok0 // 128 + tt
                          S.emit("act", lambda e, pst=pst, kb=kb: e.activation(out=V[:, kb, hp * 256:(hp + 1) * 256], in_=pst[:, 0:256], func=AF.Copy),
                                 reads=[pk], writes=[("V", hp)])
                          vst, vk = T[:, 10, tt * 128:tt * 128 + 128], "T10"
                          vst = T[:, 10, 0:256] if tt % 2 == 0 else T[:, 10, 256:512]
                          S.emit("dve", lambda e, pst=pst, vst=vst: e.tensor_copy(out=vst, in_=pst[:, 0:256]), reads=[pk], writes=["T10" + "ab"[tt % 2]])
                          S.emit("sp", lambda e, kb=kb, vst=vst: e.dma_start(out=vp[kb * 128:(kb + 1) * 128, hp * 256:(hp + 1) * 256], in_=vst),
                                 reads=["T10" + "ab"[tt % 2]], dma=True)
                      gate(2)
                      bg()
                      qT = T[:, 4, :].bitcast(BF16)
                      zaT = T[:, 5, :].bitcast(BF16)
                      wq, kq = load_w(we_in, hp * 256)
                      for m in range(2):
                          pst, pk = proj_fm(wq, kq, m, NT)
                          S.emit("act", lambda e, pst=pst, m=m: e.activation(out=qT[:, m * 512:(m + 1) * 512], in_=pst[:, 0:NT], func=AF.Copy),
                                 reads=[pk], writes=["T4" + "ab"[m]])
                      bg()
                      wza, kza = load_w(we_in, 3072 + hp * 256)
                      for m in range(2):
                          pst2, pk2 = proj_fm(wza, kza, m, NT)
                          S.emit("act", lambda e, pst2=pst2, m=m: e.activation(out=zaT[:, m * 512:(m + 1) * 512], in_=pst2[:, 0:NT], func=AF.Silu),
                                 reads=[pk2], writes=["T5" + "ab"[m]])
                      lsf = T[:, 6, :]
                      lsb = T[:, 7, :].bitcast(BF16)[:, 0:512]
                      et = T[:, 9, :]
                      tmp = T[:, 10, :]
                      items = [(m, bi, kb) for m in range(2) for bi, kb in enumerate(range(nkb - 1, -1, -1))]

                      def bufs(g):
                          sps, spk = PSt[:, g % 2, :], f"ps{g % 2}"
                          cps, cpk = PSt[:, 2 + g % 2, :], f"ps{2 + g % 2}"
                          spb = T[:, 8, :].bitcast(BF16)[:, (g % 2) * 512:(g % 2) * 512 + 512]
                          spk2 = "T8" + "ab"[g % 2]
                          wbf = T[:, 11, :].bitcast(BF16)[:, (g % 2) * 512:(g % 2) * 512 + 512]
                          wkk = "T11" + "ab"[g % 2]
                          return sps, spk, cps, cpk, spb, spk2, wbf, wkk

                      def stage_a0(g):
                          m, bi, kb = items[g]
                          h = hp * 2 + m
                          sps, spk, cps, cpk, spb, spk2, wbf, wkk = bufs(g)
                          S.emit("pe", lambda e: e.matmul(sps, KT[:, h, kb * 128:(kb + 1) * 128], qT[:, m * 512:(m + 1) * 512],
                                                          start=True, stop=True),
                                 reads=[("KT", h), "T4" + "ab"[m]], writes=[spk])

                      def stage_a1(g):
                          m, bi, kb = items[g]
                          h = hp * 2 + m
                          o = kb * 128 - tok0
                          sps, spk, cps, cpk, spb, spk2, wbf, wkk = bufs(g)
                          S.emit("act", lambda e: e.activation(out=et, in_=sps, func=AF.Exp, scale=QSCALE, bias=vcol("sbb", h)),
                                 reads=[spk, "vecT"], writes=["T9"])
                          S.emit("act", lambda e: e.activation(out=spb, in_=et, func=AF.Ln, bias=one1[:, 0:1], scale=1.0),
                                 reads=["T9", "one1"], writes=[spk2])
                          if o >= 0:
                              S.emit("pool", lambda e: e.tensor_tensor(out=spb, in0=spb, in1=maskp[:, o // 128, :], op=ALU.mult),
                                     reads=[spk2, "maskp"], writes=[spk2])
                          S.emit("pe", lambda e: e.matmul(cps, tri, spb, start=True, stop=(bi == 0)),
                                 reads=[spk2, "tri3"], writes=[cpk])
                          if bi > 0:
                              S.emit("pe", lambda e: e.matmul(cps, ones, lsb, start=False, stop=True),
                                     reads=["T7a", "tri3"], writes=[cpk])
                          if bi == 0:
                              S.emit("pool", lambda e: e.tensor_copy(out=lsf, in_=spb), reads=[spk2], writes=["T6"])
                          else:
                              S.emit("pool", lambda e: e.tensor_tensor(out=lsf, in0=lsf, in1=spb, op=ALU.add),
                                     reads=[spk2, "T6"], writes=["T6"])

                      def stage_a2(g):
                          m, bi, kb = items[g]
                          if bi < nkb - 1:
                              S.emit("dve", lambda e: e.tensor_copy(out=lsb, in_=lsf), reads=["T6"], writes=["T7a"])

                      def stage_b1(g):
                          sps, spk, cps, cpk, spb, spk2, wbf, wkk = bufs(g)
                          S.emit("dve", lambda e: e.tensor_tensor(out=tmp, in0=cps, in1=spb, op=ALU.add),
                                 reads=[cpk, spk2], writes=["T10"])
                          S.emit("dve", lambda e: e.scalar_tensor_tensor(out=tmp, in0=sps, scalar=QSCALE, in1=tmp,
                                                                         op0=ALU.mult, op1=ALU.subtract),
                                 reads=[spk, "T10"], writes=["T10"])

                      def stage_b2(g):
                          m, bi, kb = items[g]
                          h = hp * 2 + m
                          o = kb * 128 - tok0
                          oacc, oak = PSt[:, 4 + m, :], f"ps{4 + m}"
                          sps, spk, cps, cpk, spb, spk2, wbf, wkk = bufs(g)
                          S.emit("act", lambda e: e.activation(out=wbf, in_=tmp, func=AF.Exp, scale=1.0, bias=vcol("sbb", h)),
                                 reads=["T10", "vecT"], writes=[wkk])
                          if o >= 0:
                              S.emit("dve", lambda e: e.tensor_tensor(out=wbf, in0=wbf, in1=maskp[:, o // 128, :], op=ALU.mult),
                                     reads=[wkk, "maskp"], writes=[wkk])
                          S.emit("pe", lambda e: e.matmul(oacc, V[:, kb, h * 128:(h + 1) * 128], wbf, start=(bi == 0), stop=(bi == nkb - 1)),
                                 reads=[wkk, ("V", h // 2)], writes=[oak])
                          if bi == nkb - 1:
                              S.emit("dve", lambda e: e.tensor_tensor(out=mixT[:, h, :], in0=oacc, in1=zaT[:, m * 512:(m + 1) * 512], op=ALU.mult),
                                     reads=[oak, "T5" + "ab"[m]], writes=[("mixT", h)])

                      NI = len(items)
                      stage_a0(0)
                      stage_a0(1)
                      stage_a1(0)
                      stage_a2(0)
                      for g in range(NI):
                          stage_b1(g)
                          if g + 2 < NI:
                              stage_a0(g + 2)
                          if g + 1 < NI:
                              stage_a1(g + 1)
                          stage_b2(g)
                          if g + 1 < NI:
                              stage_a2(g + 1)
                  gate(4)
                  ps_banks[0] = [0, 1, 2, 3, 6, 7]
                  convb_branch(1, NT, first, last, lambda c: cbpT[c * 128:(c + 1) * 128, :])

                  gate(5)
                  bg(24)
                  def resid0(oc, tok0=tok0):
                      xs_, xk = T[:, oc % 4, :], f"T{oc % 4}"
                      S.emit("sp", lambda e: e.dma_start(out=xs_, in_=fm(xpT)[:, oc, tok0:tok0 + NT]), writes=[xk], dma=True)
                      return xs_, xk
                  outproj_ln(0, we_out, "ge_ln", "be_ln", 1, NT, 0, resid0, None, True, 1)
                  gate(6)
                  layer1(1, NT, 0, first, last,
                         lambda oc, tok0=tok0: ypT[oc * 128:(oc + 1) * 128, tok0:tok0 + NT],
                         lambda c: ccpT[c * 128:(c + 1) * 128, :],
                         lambda c: hpT[c * 128:(c + 1) * 128, :])


            if do_sample:
                NS, TS, NSM = 16, 8, 128
                gate(7)
                S.emit("sp", lambda e: e.dma_start(out=cxcar[:].rearrange("p c (s w) -> p c s w", s=16), in_=fm(scbT)),
                       writes=[("cxcar", c) for c in range(8)], dma=True)
                S.emit("sp", lambda e: e.dma_start(out=xrcar[:].rearrange("p c (s w) -> p c s w", s=16), in_=fm(sccT)),
                       writes=[("xrcar", c) for c in range(16)], dma=True)
                pti = T[:, 8, 0:256].bitcast(I32)
                ptf = T[:, 9, 0:256]
                S.emit("sp", lambda e: e.dma_start(out=pti, in_=ptab[0, :].partition_broadcast(128)), writes=["T8"], dma=True)
                S.emit("dve", lambda e: e.tensor_copy(out=ptf, in_=pti), reads=["T8"], writes=["T9"])
                S.emit("dve", lambda e: e.tensor_scalar(out=ptf, in0=ptf, scalar1=128.0, scalar2=iota[:, 0:1], op0=ALU.mult, op1=ALU.add),
                       reads=["T9", "iota"], writes=["T9"])
                S.emit("dve", lambda e: e.tensor_copy(out=idx[:], in_=ptf), reads=["T9"], writes=["idx"])
                for c in range(16):
                    xs_, xk = T[:, c % 4, 0:NSM], f"T{c % 4}"
                    S.emit("sp", lambda e: e.dma_start(out=xs_, in_=fm(xsT)[:, c, :]), writes=[xk], dma=True)
                    modulate(0, xs_, xk, c, NS, TS, 1)
                knew = R[:, 0, :].bitcast(BF16)
                vnew = R[:, 1, :].bitcast(BF16)
                qs = R[:, 2, :].bitcast(BF16)
                zas = R[:, 3, :].bitcast(BF16)
                for hp in range(4):
                    wk, kk = load_w(we_in, 1024 + hp * 256)
                    for m in range(2):
                        h = hp * 2 + m
                        pst, pk = proj_fm(wk, kk, m, NSM)
                        S.emit("act", lambda e: e.activation(out=knew[:, h * 128:(h + 1) * 128], in_=pst[:, 0:NSM], func=AF.Copy),
                               reads=[pk], writes=[("R", 0)])
                        S.emit("dve", lambda e: e.tensor_copy(out=T[:, 8 + m, 0:NSM], in_=pst[:, 0:NSM]), reads=[pk], writes=[f"T{8 + m}"])
                        S.emit("sp", lambda e: e.dma_start(out=ksT[h * 128:(h + 1) * 128, :], in_=T[:, 8 + m, 0:NSM]),
                               reads=[f"T{8 + m}"], dma=True)
                    wv, kv = load_w(we_in, 2048 + hp * 256)
                    pst, pk = ps()
                    for kc in range(16):
                        S.emit("pe", lambda e: e.matmul(pst[:, 0:256], uT[:, kc, 0:NSM], wv[:, kc, :], start=(kc == 0), stop=(kc == 15)),
                               reads=[kv, ("uT", kc)], writes=[pk])
                    S.emit("act", lambda e: e.activation(out=vnew[:, hp * 256:(hp + 1) * 256], in_=pst[:, 0:256], func=AF.Copy),
                           reads=[pk], writes=[("R", 1)])
                    S.emit("dve", lambda e: e.tensor_copy(out=T[:, 10, 0:256], in_=pst[:, 0:256]), reads=[pk], writes=["T10"])
                    S.emit("sp", lambda e: e.dma_start(out=vs[:, hp * 256:(hp + 1) * 256], in_=T[:, 10, 0:256]), reads=["T10"], dma=True)
                    wq, kq = load_w(we_in, hp * 256)
                    for m in range(2):
                        h = hp * 2 + m
                        pst, pk = proj_fm(wq, kq, m, NSM)
                        S.emit("act", lambda e: e.activation(out=qs[:, h * 128:(h + 1) * 128], in_=pst[:, 0:NSM], func=AF.Copy),
                               reads=[pk], writes=[("R", 2)])
                    wza, kza = load_w(we_in, 3072 + hp * 256)
                    for m in range(2):
                        h = hp * 2 + m
                        pst2, pk2 = proj_fm(wza, kza, m, NSM)
                        S.emit("act", lambda e: e.activation(out=zas[:, h * 128:(h + 1) * 128], in_=pst2[:, 0:NSM], func=AF.Silu),
                               reads=[pk2], writes=[("R", 3)])
                gate(8)
                ps_banks[0] = [6, 7]
                NCOL = 17 * 64
                zps = PSt[:, 0:3, :].rearrange("p a b -> p (a b)")
                cps = PSt[:, 3:6, :].rearrange("p a b -> p (a b)")
                Tf = T[:].rearrange("p a b -> p (a b)")
                et = Tf[:, 0:NCOL]
                lsf = Tf[:, 3 * 512:3 * 512 + NCOL]
                spb = Tf[:, 6 * 512:8 * 512].bitcast(BF16)[:, 0:NCOL]
                Rf = R[:].rearrange("p a b -> p (a b)")
                wbf = Rf[:, 12 * 512:14 * 512].bitcast(BF16)[:, 0:NCOL]
                lsb = Rf[:, 14 * 512:16 * 512].bitcast(BF16)[:, 0:NCOL]
                K_ET, K_LSF, K_SPB = ["T0", "T1", "T2"], ["T3", "T4", "T5"], ["T6", "T7"]
                K_WBF, K_LSB = [("R", 12), ("R", 13)], [("R", 14), ("R", 15)]
                K_ZPS, K_CPS = ["ps0", "ps1", "ps2"], ["ps3", "ps4", "ps5"]
                qs3 = qs.rearrange("p (h t) -> p h t", h=8)
                zas3 = zas.rearrange("p (h t) -> p h t", h=8)
                def part_k(s_):
                    ksts = []
                    for pg in range(16):
                        slot = 4 + pg % 8
                        kst = R[:, slot, :].bitcast(BF16)
                        S.emit("pool", lambda e: e.indirect_dma_start(
                            out=kst, out_offset=None, in_=cache_k,
                            in_offset=bass.IndirectOffsetOnAxis(ap=idx[:, s_ * 16 + pg:s_ * 16 + pg + 1], axis=0)),
                            reads=["idx"], writes=[("R", slot)], dma=True)
                        bank = 6 + pg % 2
                        ptb = PSt[:, bank, :].bitcast(BF16)
                        for h in range(8):
                            S.emit("pe", lambda e: e.transpose(ptb[:, h * 128:(h + 1) * 128], kst[:, h * 128:(h + 1) * 128], ident),
                                   reads=[("R", slot), "tri3"], writes=[f"ps{bank}"])
                        ev = "act" if pg % 2 == 0 else "dve"
                        if ev == "act":
                            S.emit("act", lambda e: e.activation(out=KT[:, :, pg * 128:(pg + 1) * 128],
                                                                 in_=ptb.rearrange("p (h k) -> p h k", h=8), func=AF.Copy),
                                   reads=[f"ps{bank}"], writes=[("KT", h) for h in range(8)])
                        else:
                            S.emit("dve", lambda e: e.tensor_copy(out=KT[:, :, pg * 128:(pg + 1) * 128],
                                                                  in_=ptb.rearrange("p (h k) -> p h k", h=8)),
                                   reads=[f"ps{bank}"], writes=[("KT", h) for h in range(8)])

                def part_v(s_):
                    for pg in range(16):
                        S.emit("pool", lambda e: e.indirect_dma_start(
                            out=V[:, pg, :], out_offset=None, in_=cache_v,
                            in_offset=bass.IndirectOffsetOnAxis(ap=idx[:, s_ * 16 + pg:s_ * 16 + pg + 1], axis=0)),
                            reads=["idx"], writes=[("Vp", pg)], dma=True)

                def part_s(s_):
                    for pg in range(17):
                        for h in range(8):
                            lhs = KT[:, h, pg * 128:(pg + 1) * 128] if pg < 16 else knew[:, h * 128:(h + 1) * 128]
                            rk = [("KT", h)] if pg < 16 else [("R", 0)]
                            c0 = pg * 64 + h * 8
                            S.emit("pe", lambda e: e.matmul(zps[:, c0:c0 + 8], lhs, qs3[:, h, s_ * 8:(s_ + 1) * 8], start=True, stop=True),
                                   reads=rk + [("R", 2)], writes=[f"ps{c0 // 512}"])
                    zps4 = zps[:, 0:NCOL].rearrange("p (g h q) -> p g h q", g=17, h=8)
                    et4 = et.rearrange("p (g h q) -> p g h q", g=17, h=8)
                    for h in range(8):
                        S.emit("act", lambda e: e.activation(out=et4[:, :, h, :], in_=zps4[:, :, h, :], func=AF.Exp,
                                                             scale=QSCALE, bias=vcol("sbb", h)),
                               reads=K_ZPS + ["vecT"], writes=K_ET)
                    S.emit("act", lambda e: e.activation(out=spb, in_=et, func=AF.Ln, bias=one1[:, 0:1], scale=1.0),
                           reads=K_ET + ["one1"], writes=K_SPB)
                    mk = masknew[:, s_, :].unsqueeze(1).to_broadcast([128, 8, 8])
                    spn = spb[:, 1024:NCOL].rearrange("p (h q) -> p h q", h=8)
                    S.emit("dve", lambda e: e.tensor_tensor(out=spn, in0=spn, in1=mk, op=ALU.mult),
                           reads=K_SPB + ["masknew"], writes=K_SPB)

                def part_e(s_):
                    mk = masknew[:, s_, :].unsqueeze(1).to_broadcast([128, 8, 8])
                    S.emit("dve", lambda e: e.memset(lsf[:, 1024:NCOL], 0.0), writes=K_LSF)
                    for pg in range(15, -1, -1):
                        S.emit("dve", lambda e: e.tensor_tensor(out=lsf[:, pg * 64:(pg + 1) * 64], in0=lsf[:, (pg + 1) * 64:(pg + 2) * 64],
                                                                in1=spb[:, (pg + 1) * 64:(pg + 2) * 64], op=ALU.add),
                               reads=K_LSF + K_SPB, writes=K_LSF)
                    S.emit("act", lambda e: e.activation(out=lsb, in_=lsf, func=AF.Copy), reads=K_LSF, writes=K_LSB)
                    for (c0, c1) in [(0, 512), (512, 1024), (1024, NCOL)]:
                        S.emit("pe", lambda e: e.matmul(cps[:, c0:c1], tri, spb[:, c0:c1], start=True, stop=False),
                               reads=K_SPB + ["tri3"], writes=[f"ps{3 + c0 // 512}"])
                        S.emit("pe", lambda e: e.matmul(cps[:, c0:c1], ones, lsb[:, c0:c1], start=False, stop=True),
                               reads=K_LSB + ["tri3"], writes=[f"ps{3 + c0 // 512}"])
                    S.emit("dve", lambda e: e.tensor_tensor(out=lsf, in0=cps[:, 0:NCOL], in1=spb, op=ALU.add),
                           reads=K_CPS + K_SPB, writes=K_LSF)
                    S.emit("act", lambda e: e.activation(out=lsf, in_=lsf, func=AF.Exp, scale=-1.0), reads=K_LSF, writes=K_LSF)
                    S.emit("dve", lambda e: e.tensor_tensor(out=wbf, in0=et, in1=lsf, op=ALU.mult), reads=K_ET + K_LSF, writes=K_WBF)
                    wn = wbf[:, 1024:NCOL].rearrange("p (h q) -> p h q", h=8)
                    S.emit("dve", lambda e: e.tensor_tensor(out=wn, in0=wn, in1=mk, op=ALU.mult), reads=K_WBF + ["masknew"], writes=K_WBF)
                    ops_, opk = PSt[:, 7, :], "ps7"
                    for h in range(8):
                        for pg in range(17):
                            lhs = V[:, pg, h * 128:(h + 1) * 128] if pg < 16 else vnew[:, h * 128:(h + 1) * 128]
                            rk = [("Vp", pg)] if pg < 16 else [("R", 1)]
                            c0 = pg * 64 + h * 8
                            S.emit("pe", lambda e: e.matmul(ops_[:, h * 8:(h + 1) * 8], lhs, wbf[:, c0:c0 + 8], start=(pg == 0), stop=(pg == 16)),
                                   reads=rk + K_WBF, writes=[opk])
                    S.emit("dve", lambda e: e.tensor_tensor(out=mixT[:, 0:8, s_ * 8:(s_ + 1) * 8],
                                                            in0=ops_[:, 0:64].rearrange("p (h q) -> p h q", h=8),
                                                            in1=zas3[:, :, s_ * 8:(s_ + 1) * 8], op=ALU.mult),
                           reads=[opk, ("R", 3)], writes=[("mixT", h) for h in range(8)])

                part_k(0)
                part_s(0)
                part_v(0)
                for s_ in range(NS):
                    if s_ + 1 < NS:
                        part_k(s_ + 1)
                    part_e(s_)
                    if s_ + 1 < NS:
                        part_s(s_ + 1)
                        part_v(s_ + 1)
                gate(9)
                ps_banks[0] = [0, 1, 2, 3, 6, 7]
                convb_branch(NS, TS, False, True, lambda c: cbsT[c * 128:(c + 1) * 128, :, :].rearrange("p s w -> p (s w)"))

                def resid_s(oc):
                    xs_, xk = T[:, oc % 4, 0:NSM], f"T{oc % 4}"
                    S.emit("sp", lambda e: e.dma_start(out=xs_, in_=fm(xsT)[:, oc, :]), writes=[xk], dma=True)
                    return xs_, xk
                outproj_ln(0, we_out, "ge_ln", "be_ln", NS, TS, 1, resid_s, None, True, 1)
                gate(10)
                layer1(NS, TS, 1, None, True,
                       lambda oc: ysT[oc * 128:(oc + 1) * 128, :],
                       lambda c: ccsT[c * 128:(c + 1) * 128, :, :].rearrange("p s w -> p (s w)"),
                       lambda c: hsT[c * 128:(c + 1) * 128, :])
        except _Stop:
            pass
        S.finish("sp")
        S.replay()
    nc._in_names = in_names
    if wseq is None:
        return build(do_prompt, do_sample, npass, stage, wseq=wrec)
    return nc


_cache = {}


def _get_nc():
    if "nc" not in _cache:
        _cache["nc"] = build()
    return _cache["nc"]


def _w_layout(w):
    n = w.shape[1]
    return np.ascontiguousarray(w.reshape(16, 128, n).transpose(1, 0, 2))


def _colv(v):
    return np.asarray(v, dtype=np.float32).reshape(-1, 128).T


def prep_inputs(inp):
    f = lambda a: np.asarray(a, dtype=np.float32)
    shared = {}
    shared["cache_k"] = f(inp["cache_k"]).reshape(NPHYS * 128, 1024)
    shared["cache_v"] = f(inp["cache_v"]).reshape(NPHYS * 128, 1024)
    shared["we_ada"] = _w_layout(f(inp["we_ada"])[0])
    shared["we_in"] = _w_layout(f(inp["we_in"])[0])
    shared["we_out"] = _w_layout(f(inp["we_out"])[0])
    shared["wo_ada"] = _w_layout(f(inp["wo_ada"])[0])
    shared["wo_in"] = _w_layout(f(inp["wo_in"])[0])
    shared["wo_out"] = _w_layout(f(inp["wo_out"])[0])
    shared["wga"] = np.ascontiguousarray(f(inp["wo_gate_a"])[0].reshape(8, 2, 128, 256).transpose(2, 0, 1, 3))
    shared["wgx"] = np.ascontiguousarray(f(inp["wo_gate_x"])[0].reshape(8, 2, 128, 256).transpose(2, 0, 1, 3))
    vec = [_colv(f(inp["be_ada"])[0]), _colv(f(inp["bo_ada"])[0]), _colv(f(inp["ge_ln"])[0]), _colv(f(inp["be_ln"])[0]),
           _colv(f(inp["go_ln"])[0]), _colv(f(inp["bo_ln"])[0]), _colv(f(inp["we_conv"])[0].reshape(-1)),
           _colv(f(inp["wo_conv"])[0].reshape(-1)), _colv(f(inp["bo_conv"])[0]), _colv(f(inp["bo_gate_a"])[0]),
           _colv(f(inp["bo_gate_x"])[0]), _colv(f(inp["wo_lambda"])[0]),
           np.broadcast_to(f(inp["we_sb_bias"])[0][None, :], (128, 8))]
    shared["vecT"] = np.ascontiguousarray(np.concatenate(vec, axis=1))
    assert shared["vecT"].shape == (128, NV)
    j = np.arange(128)
    tri = (j[:, None] > j[None, :]).astype(np.float32)
    ones = np.ones((128, 128), np.float32)
    ident = np.eye(128, dtype=np.float32)
    shared["c_tri"] = np.ascontiguousarray(np.stack([tri, ones, ident], axis=1))
    tq = np.arange(512)
    shared["c_maskp"] = np.ascontiguousarray(np.stack(
        [(tq[None, :] > (j[:, None] + 128 * o)).astype(np.float32) for o in range(4)], axis=1))
    s_ = np.arange(16)
    q_ = np.arange(8)
    shared["c_masknew"] = np.ascontiguousarray(
        ((j[:, None, None] // 8 == s_[None, :, None]) & (j[:, None, None] % 8 < q_[None, None, :])).astype(np.float32))
    shared["c_iota"] = j.astype(np.float32).reshape(128, 1)
    xp = f(inp["x_prompt"])
    xs = f(inp["x_sample"])
    cp = f(inp["c_prompt"])
    cs = f(inp["c_sample"])
    pt = np.asarray(inp["page_table"], dtype=np.int32)
    scb = f(inp["state_conv_b"])[0]
    scc = f(inp["state_conv_c"])[0]
    sh = f(inp["state_h"])[0]
    maps = []
    for c in range(NCORES):
        b = c % 4
        sl = slice(c * 16, (c + 1) * 16)
        m = dict(shared)
        m["xpT"] = np.ascontiguousarray(xp[b].T)
        m["xsT"] = np.ascontiguousarray(xs[sl].reshape(128, D).T)
        m["cT"] = np.ascontiguousarray(np.concatenate([cp[b:b + 1], cs[sl]], axis=0).T)
        m["ptab"] = np.ascontiguousarray(pt[sl].reshape(1, 256))
        m["scbT"] = np.ascontiguousarray(scb[sl].transpose(2, 0, 1))
        m["sccT"] = np.ascontiguousarray(scc[sl].transpose(2, 0, 1))
        m["shT"] = np.ascontiguousarray(sh[sl].T)
        maps.append(m)
    return maps


def assemble(results):
    B, SEQ = 4, 2048
    yp = np.stack([results[b]["ypT"].T for b in range(B)])
    ys = np.concatenate([results[c]["ysT"].T.reshape(16, 8, D) for c in range(NCORES)], axis=0)
    kp = np.stack([results[b]["kpT"].T.reshape(SEQ, 8, 128) for b in range(B)])[None]
    vp = np.stack([results[b]["vp"].reshape(SEQ, 8, 128) for b in range(B)])[None]
    cbp = np.stack([results[b]["cbpT"].T for b in range(B)])[None]
    ccp = np.stack([results[b]["ccpT"].T for b in range(B)])[None]
    hp = np.stack([results[b]["hpT"][:, 0] for b in range(B)])[None]
    ks = np.concatenate([results[c]["ksT"].T.reshape(16, 8, 8, 128) for c in range(NCORES)], axis=0)[None]
    vs = np.concatenate([results[c]["vs"].reshape(16, 8, 8, 128) for c in range(NCORES)], axis=0)[None]
    cbs = np.concatenate([results[c]["cbsT"].transpose(1, 2, 0) for c in range(NCORES)], axis=0)[None]
    ccs = np.concatenate([results[c]["ccsT"].transpose(1, 2, 0) for c in range(NCORES)], axis=0)[None]
    hs = np.concatenate([results[c]["hsT"].T for c in range(NCORES)], axis=0)[None]
    outs = (yp, ys, kp, vp, cbp, ccp, hp, ks, vs, cbs, ccs, hs)
    return tuple(np.ascontiguousarray(o, dtype=np.float32) for o in outs)


def kernel(**inputs):
    nc = _get_nc()
    maps = prep_inputs(inputs)
    res = run_bass_kernel_spmd(nc, maps, core_ids=list(range(NCORES)))
    return assemble(res.results)
```

```python
import contextlib
import numpy as np
import concourse.bass as bass
import concourse.mybir as mybir
from concourse.bass_utils import run_bass_kernel_spmd

F32 = mybir.dt.float32
BF16 = mybir.dt.bfloat16
I32 = mybir.dt.int32
AF = mybir.ActivationFunctionType
ALU = mybir.AluOpType

SAME_ENGINE_SYNC = True
NCORES = 8
D = 2048
NT = 512
NPASS = 4
ALPHA = float((2.0 * 2) ** 0.25)
QSCALE = float(1.0 / np.sqrt(128.0))
EPS = 1e-5
NPHYS = 2560

VO = {}
_off = 0
for _nm, _n in [("be_ada", 48), ("bo_ada", 48), ("ge_ln", 16), ("be_ln", 16), ("go_ln", 16),
                ("bo_ln", 16), ("we_conv", 24), ("wo_conv", 64), ("bo_conv", 16), ("bga", 16),
                ("bgx", 16), ("lam", 16), ("sbb", 8)]:
    VO[_nm] = _off
    _off += _n
NV = _off


class _Rec:
    def __init__(self):
        self.call = None

    def __getattr__(self, name):
        def f(*a, **k):
            self.call = (name, a, k)
            return None
        return f


class Sched:
    ENG = ("pe", "act", "dve", "pool", "sp")

    def __init__(self, nc):
        self.nc = nc
        self.ops = {e: [] for e in self.ENG}
        self.n = {e: 0 for e in self.ENG}
        self.flag = {e: set() for e in self.ENG}
        self.dcount = {}
        self.lastw = {}
        self.readers = {}
        self.seen = {e: {} for e in self.ENG}
        self.dpool = {"sp": [f"s{i}" for i in range(14)], "pool": [f"g{i}" for i in range(14)]}
        self.dnext = {"sp": 0, "pool": 0}
        self.lastdma = {}

    def _need(self, eng, tok):
        kind, key, idx = tok
        if kind == "e" and key == eng:
            if eng == "pe" or not SAME_ENGINE_SYNC:
                return
        k = (kind, key)
        if self.seen[eng].get(k, 0) >= idx:
            return
        self.seen[eng][k] = idx
        if kind == "e":
            self.flag[key].add(idx)
        self.ops[eng].append(("wait", tok))

    @staticmethod
    def _exp(keys):
        out = []
        for k in keys:
            if isinstance(k, str) and k[0] == "T" and k[1:].isdigit():
                out += [k + "a", k + "b"]
            else:
                out.append(k)
        return out

    def emit(self, eng, fn, reads=(), writes=(), dma=False):
        reads = self._exp(reads)
        writes = self._exp(writes)
        writes = writes + [k for k in reads if isinstance(k, str) and k.startswith("ps") and k not in writes]
        rec = _Rec()
        fn(rec)
        call = rec.call
        fn = lambda eo, call=call: getattr(eo, call[0])(*call[1], **call[2])
        deps = []
        for r in reads:
            if r in self.lastw:
                deps.append(self.lastw[r])
        for w in writes:
            if w in self.lastw:
                deps.append(self.lastw[w])
            deps.extend(self.readers.get(w, {}).values())
        dname = None
        if dma:
            dname = self.dpool[eng][self.dnext[eng] % len(self.dpool[eng])]
            self.dnext[eng] += 1
            if dname in self.lastdma:
                deps.append(self.lastdma[dname])
        for t in deps:
            self._need(eng, t)
        if not dma:
            self.n[eng] += 1
            tok = ("e", eng, self.n[eng])
            self.ops[eng].append(("op", fn, self.n[eng]))
        else:
            self.dcount[dname] = self.dcount.get(dname, 0) + 16
            tok = ("d", dname, self.dcount[dname])
            self.lastdma[dname] = tok
            self.ops[eng].append(("dma", fn, dname))
        for r in reads:
            self.readers.setdefault(r, {})[(tok[0], tok[1])] = tok
        for w in writes:
            self.lastw[w] = tok
            self.readers[w] = {}
        return tok

    def finish(self, eng="sp"):
        for name, val in self.dcount.items():
            self._need(eng, ("d", name, val))
        for e in self.ENG:
            if e != eng and self.n[e] > 0:
                self._need(eng, ("e", e, self.n[e]))

    def replay(self):
        nc = self.nc
        engobj = {"pe": nc.tensor, "act": nc.scalar, "dve": nc.vector, "pool": nc.gpsimd, "sp": nc.sync}
        sem_names = ["e_" + e for e in self.ENG] + ["d_" + d for d in self.dcount]
        with contextlib.ExitStack() as st:
            sems = {nm: st.enter_context(nc.semaphore(nm)) for nm in sem_names}
            pref = {}
            for e in self.ENG:
                arr = np.zeros(self.n[e] + 1, dtype=np.int64)
                for i in self.flag[e]:
                    arr[i] = 1
                pref[e] = np.cumsum(arr)
            with nc.Block() as block:
                def run(e, eo):
                    for op in self.ops[e]:
                        if op[0] == "wait":
                            kind, key, idx = op[1]
                            if kind == "e":
                                eo.wait_ge(sems["e_" + key], int(pref[key][idx]))
                            else:
                                eo.wait_ge(sems["d_" + key], int(idx))
                        elif op[0] == "op":
                            inst = op[1](eo)
                            if op[2] in self.flag[e]:
                                inst.then_inc(sems["e_" + e], 1)
                        else:
                            inst = op[1](eo)
                            inst.then_inc(sems["d_" + op[2]], 16)

                @block.tensor
                def _(t):
                    run("pe", t)

                @block.scalar
                def _(t):
                    run("act", t)

                @block.vector
                def _(t):
                    run("dve", t)

                @block.gpsimd
                def _(t):
                    run("pool", t)

                @block.sync
                def _(t):
                    run("sp", t)


class _Stop(Exception):
    pass


NWB = 3


def build(do_prompt=True, do_sample=True, npass=NPASS, stage=99, wseq=None):
    nc = bass.Bass("TRN2", target_bir_lowering=False)
    S = Sched(nc)

    in_names = []

    def din(name, shape, dt=F32):
        in_names.append(name)
        return nc.dram_tensor(name, list(shape), dt, kind="ExternalInput").ap()

    def dout(name, shape, dt=F32):
        return nc.dram_tensor(name, list(shape), dt, kind="ExternalOutput").ap()

    xpT = din("xpT", [D, 2048])
    xsT = din("xsT", [D, 128])
    cT = din("cT", [D, 17])
    ptab = din("ptab", [1, 256], I32)
    if do_sample:
        cache_k = din("cache_k", [NPHYS * 128, 1024])
        cache_v = din("cache_v", [NPHYS * 128, 1024])
    scbT = din("scbT", [1024, 16, 2])
    sccT = din("sccT", [D, 16, 3])
    shT = din("shT", [D, 16])
    we_ada = din("we_ada", [128, 16, 6144])
    we_in = din("we_in", [128, 16, 8192])
    we_out = din("we_out", [128, 16, 2048])
    wo_ada = din("wo_ada", [128, 16, 6144])
    wo_in = din("wo_in", [128, 16, 4096])
    wo_out = din("wo_out", [128, 16, 2048])
    wga = din("wga", [128, 8, 2, 256])
    wgx = din("wgx", [128, 8, 2, 256])
    wmap = {"we_ada": we_ada, "we_in": we_in, "we_out": we_out, "wo_ada": wo_ada, "wo_in": wo_in, "wo_out": wo_out}
    vecT_d = din("vecT", [128, NV])
    c_tri = din("c_tri", [128, 3, 128])
    c_maskp = din("c_maskp", [128, 4, 512])
    c_masknew = din("c_masknew", [128, 16, 8])
    c_iota = din("c_iota", [128, 1])

    ypT = dout("ypT", [D, 2048])
    ysT = dout("ysT", [D, 128])
    kpT = dout("kpT", [1024, 2048])
    vp = dout("vp", [2048, 1024])
    cbpT = dout("cbpT", [1024, 2])
    ccpT = dout("ccpT", [D, 3])
    hpT = dout("hpT", [D, 1])
    ksT = dout("ksT", [1024, 128])
    vs = dout("vs", [128, 1024])
    cbsT = dout("cbsT", [1024, 16, 2])
    ccsT = dout("ccsT", [D, 16, 3])
    hsT = dout("hsT", [D, 16])

    def fm(ap):
        if len(ap.shape) == 2:
            return ap.rearrange("(c p) t -> p c t", p=128)
        return ap.rearrange("(c p) s t -> p c s t", p=128)

    st = contextlib.ExitStack()
    with st:
        def sb(name, shape, dt=F32):
            return st.enter_context(nc.sbuf_tensor(name, list(shape), dt))

        KT = sb("KT", [128, 8, 2048], BF16)
        V = sb("V", [128, 16, 1024], BF16)
        R = sb("R", [128, 16, 512], F32)
        uT = sb("uT", [128, 16, 512], BF16)
        mixT = sb("mixT", [128, 16, 512], BF16)
        T = sb("T", [128, 12, 512], F32)
        WB = [sb(f"wb{i}", [128, 16, 256], BF16) for i in range(NWB)]
        GW = [sb(f"gw{i}", [128, 2, 256], BF16) for i in range(4)]
        adaT = sb("adaT", [128, 2, 48, 17], F32)
        siluT = sb("siluT", [128, 16, 17], BF16)
        vecT = sb("vecT_sb", [128, NV], F32)
        tri3 = sb("tri3", [128, 3, 128], BF16)
        maskp = sb("maskp", [128, 4, 512], BF16)
        masknew = sb("masknew", [128, 16, 8], F32)
        iota = sb("iota", [128, 1], F32)
        hcar = sb("hcar", [128, 16], F32)
        clam = sb("clam", [128, 16], F32)
        hv = sb("hv", [128, 48], F32)
        one1 = sb("one1", [128, 1], F32)
        cxe = sb("cxe", [128, 2, 516], F32)
        cxcar = sb("cxcar", [128, 8, 32], F32)
        xre = sb("xre", [128, 2, 516], F32)
        xrcar = sb("xrcar", [128, 16, 48], F32)
        idx = sb("idx", [128, 256], I32)
        PSt = st.enter_context(nc.psum_tensor("PSt", [128, 8, 512], F32))

        tri = tri3[:, 0, :]
        ones = tri3[:, 1, :]
        ident = tri3[:, 2, :]

        wb_i = [0, 0]
        wrec = []

        def load_w(wdram, col0):
            k = wb_i[0]
            wb_i[0] += 1
            if wseq is None:
                wrec.append((wdram.tensor.name, col0))
                i = k % NWB
                S.emit("pool", lambda e: e.dma_start(out=WB[i][:], in_=wdram[:, :, col0:col0 + 256]),
                       writes=[f"wb{i}"], dma=True)
                return WB[i], f"wb{i}"
            assert wseq[k] == (wdram.tensor.name, col0), (k, col0)
            while wb_i[1] < min(len(wseq), k + NWB):
                j = wb_i[1]
                wb_i[1] += 1
                wd, col_j = wmap[wseq[j][0]], wseq[j][1]
                S.emit("pool", lambda e: e.dma_start(out=WB[j % NWB][:], in_=wd[:, :, col_j:col_j + 256]),
                       writes=[f"wb{j % NWB}"], dma=True)
            return WB[k % NWB], f"wb{k % NWB}"

        ps_i = [0]
        ps_banks = [[0, 1, 2, 3, 6, 7]]

        def ps():
            b = ps_banks[0]
            i = b[ps_i[0] % len(b)]
            ps_i[0] += 1
            return PSt[:, i, :], f"ps{i}"

        def vcol(name, j):
            o = VO[name] + j
            return vecT[:, o:o + 1]

        S.emit("sp", lambda e: e.dma_start(out=vecT[:], in_=vecT_d), writes=["vecT"], dma=True)
        S.emit("pool", lambda e: e.dma_start(out=tri3[:], in_=c_tri), writes=["tri3"], dma=True)
        S.emit("pool", lambda e: e.dma_start(out=maskp[:], in_=c_maskp), writes=["maskp"], dma=True)
        S.emit("sp", lambda e: e.dma_start(out=masknew[:], in_=c_masknew), writes=["masknew"], dma=True)
        S.emit("sp", lambda e: e.dma_start(out=iota[:], in_=c_iota), writes=["iota"], dma=True)
        S.emit("dve", lambda e: e.memset(one1[:], 1.0), writes=["one1"])
        lam = vecT[:, VO["lam"]:VO["lam"] + 16]
        S.emit("act", lambda e: e.activation(out=clam[:], in_=lam, func=AF.Exp, scale=-1.0),
               reads=["vecT"], writes=["clam"])
        S.emit("act", lambda e: e.activation(out=clam[:], in_=clam[:], func=AF.Ln, bias=one1[:, 0:1], scale=1.0),
               reads=["clam", "one1"], writes=["clam"])
        S.emit("dve", lambda e: e.tensor_scalar(out=clam[:], in0=clam[:], scalar1=-8.0, scalar2=None, op0=ALU.mult),
               reads=["clam"], writes=["clam"])
        S.emit("dve", lambda e: e.tensor_scalar(out=hv[:, 0:16], in0=clam[:], scalar1=0.5, scalar2=None, op0=ALU.mult),
               reads=["clam"], writes=["hv"])
        S.emit("dve", lambda e: e.tensor_scalar(out=hv[:, 16:32], in0=vecT[:, VO["bga"]:VO["bga"] + 16], scalar1=0.5, scalar2=None, op0=ALU.mult),
               reads=["vecT"], writes=["hv"])
        S.emit("dve", lambda e: e.tensor_scalar(out=hv[:, 32:48], in0=vecT[:, VO["bgx"]:VO["bgx"] + 16], scalar1=0.5, scalar2=None, op0=ALU.mult),
               reads=["vecT"], writes=["hv"])

        cTs = T[:, 0, 0:16 * 17].rearrange("p (c s) -> p c s", c=16)
        S.emit("sp", lambda e: e.dma_start(out=cTs, in_=fm(cT)), writes=["T0"], dma=True)
        S.emit("act", lambda e: e.activation(out=siluT[:], in_=cTs, func=AF.Silu), reads=["T0"], writes=["siluT"])
        def ada_block(l, wada, bname, blk):
            wbuf, wkey = load_w(wada, blk * 256)
            for m in range(2):
                ch = blk * 2 + m
                pst, pk = ps()
                for kc in range(16):
                    S.emit("pe", lambda e: e.matmul(
                        pst[:, 0:17], wbuf[:, kc, m * 128:(m + 1) * 128], siluT[:, kc, :],
                        start=(kc == 0), stop=(kc == 15)), reads=[wkey, "siluT"], writes=[pk])
                addc = 1.0 if ch >= 16 else 0.0
                S.emit("dve", lambda e: e.tensor_scalar(
                    out=adaT[:, l, ch, :], in0=pst[:, 0:17], scalar1=vcol(bname, ch), scalar2=addc,
                    op0=ALU.add, op1=ALU.add), reads=[pk, "vecT"], writes=[("adaT", l)])
                if ch >= 32:
                    S.emit("dve", lambda e: e.tensor_scalar(out=adaT[:, l, ch, :], in0=adaT[:, l, ch, :], scalar1=1.0 / ALPHA,
                                                            scalar2=None, op0=ALU.mult), reads=[("adaT", l)], writes=[("adaT", l)])

        for blk in range(24):
            ada_block(0, we_ada, "be_ada", blk)
        bg_tasks = [(lambda blk=blk: ada_block(1, wo_ada, "bo_ada", blk)) for blk in range(24)]

        def bg(k=1):
            for _ in range(k):
                if bg_tasks:
                    bg_tasks.pop(0)()

        if not do_prompt:
            bg(24)

        def modulate(l, src_chunk, src_key, c, nseq, t, col0):
            n = nseq * t
            if nseq == 1:
                S.emit("act", lambda e: e.activation(out=uT[:, c, 0:n], in_=src_chunk, func=AF.Identity,
                                                     scale=adaT[:, l, 16 + c, col0:col0 + 1], bias=adaT[:, l, c, col0:col0 + 1]),
                       reads=[src_key, ("adaT", l)], writes=[("uT", c)])
                return
            sc = adaT[:, l, 16 + c, col0:col0 + nseq].unsqueeze(2).to_broadcast([128, nseq, t])
            sh = adaT[:, l, c, col0:col0 + nseq].unsqueeze(2).to_broadcast([128, nseq, t])
            tmp = T[:, 11, 0:n].rearrange("p (s t) -> p s t", s=nseq)
            srcv = src_chunk.rearrange("p (s t) -> p s t", s=nseq)
            S.emit("dve", lambda e: e.tensor_tensor(out=tmp, in0=srcv, in1=sc, op=ALU.mult),
                   reads=[src_key, ("adaT", l)], writes=["T11"])
            S.emit("dve", lambda e: e.tensor_tensor(out=uT[:, c, 0:n].rearrange("p (s t) -> p s t", s=nseq),
                                                    in0=tmp, in1=sh, op=ALU.add),
                   reads=["T11", ("adaT", l)], writes=[("uT", c)])

        def proj_fm(wbuf, wkey, m, n, src=None, skeys=None):
            pst, pk = ps()
            for kc in range(16):
                S.emit("pe", lambda e, kc=kc: e.matmul(pst[:, 0:n], wbuf[:, kc, m * 128:(m + 1) * 128],
                                                       uT[:, kc, 0:n], start=(kc == 0), stop=(kc == 15)),
                       reads=[wkey, ("uT", kc)], writes=[pk])
            return pst, pk

        def outproj_ln(l, wout, gname, bname, nseq, t, col0, resid_loader, ydst, ydst_is_R, next_l):
            n = nseq * t
            ssum = PSt[:, 4, :]
            ssq = PSt[:, 5, :]
            for blk in range(8):
                wbuf, wkey = load_w(wout, blk * 256)
                for m in range(2):
                    oc = blk * 2 + m
                    pst, pk = ps()
                    for kc in range(16):
                        S.emit("pe", lambda e, kc=kc, wbuf=wbuf, m=m, pst=pst: e.matmul(
                            pst[:, 0:n], wbuf[:, kc, m * 128:(m + 1) * 128], mixT[:, kc, 0:n],
                            start=(kc == 0), stop=(kc == 15)), reads=[wkey, ("mixT", kc)], writes=[pk])
                    xres, xkey = resid_loader(oc)
                    if nseq == 1:
                        S.emit("dve", lambda e: e.scalar_tensor_tensor(
                            out=R[:, oc, 0:n], in0=pst[:, 0:n], scalar=adaT[:, l, 32 + oc, col0:col0 + 1], in1=xres,
                            op0=ALU.mult, op1=ALU.add), reads=[pk, ("adaT", l), xkey], writes=[("R", oc)])
                    else:
                        g1 = adaT[:, l, 32 + oc, col0:col0 + nseq].unsqueeze(2).to_broadcast([128, nseq, t])
                        tmp = T[:, 10, 0:n]
                        S.emit("dve", lambda e: e.tensor_tensor(
                            out=tmp.rearrange("p (s t) -> p s t", s=nseq),
                            in0=pst[:, 0:n].rearrange("p (s t) -> p s t", s=nseq), in1=g1, op=ALU.mult),
                            reads=[pk, ("adaT", l)], writes=["T10"])
                        S.emit("dve", lambda e: e.tensor_tensor(out=R[:, oc, 0:n], in0=xres, in1=tmp, op=ALU.add),
                               reads=[xkey, "T10"], writes=[("R", oc)])
                    rb = T[:, 8, 0:n // 2].bitcast(BF16) if False else None
                    rbf = T[:, 8, :].bitcast(BF16)[:, 0:n]
                    rsq = T[:, 9, :].bitcast(BF16)[:, 0:n]
                    S.emit("act", lambda e, oc=oc, rbf=rbf: e.activation(out=rbf, in_=R[:, oc, 0:n], func=AF.Copy),
                           reads=[("R", oc)], writes=["T8"])
                    S.emit("act", lambda e, oc=oc, rsq=rsq: e.activation(out=rsq, in_=R[:, oc, 0:n], func=AF.Square),
                           reads=[("R", oc)], writes=["T9"])
                    S.emit("pe", lambda e, oc=oc, rbf=rbf: e.matmul(ssum[:, 0:n], ones, rbf, start=(oc == 0), stop=(oc == 15)),
                           reads=["T8", "tri3"], writes=["ps4"])
                    S.emit("pe", lambda e, oc=oc, rsq=rsq: e.matmul(ssq[:, 0:n], ones, rsq, start=(oc == 0), stop=(oc == 15)),
                           reads=["T9", "tri3"], writes=["ps5"])
            mean = T[:, 8, 0:n]
            rstd = T[:, 9, 0:n]
            nmr = T[:, 10, 0:n]
            S.emit("dve", lambda e: e.tensor_scalar(out=mean, in0=ssum[:, 0:n], scalar1=1.0 / D, scalar2=None, op0=ALU.mult),
                   reads=["ps4"], writes=["T8"])
            S.emit("dve", lambda e: e.tensor_tensor(out=nmr, in0=mean, in1=mean, op=ALU.mult), reads=["T8"], writes=["T10"])
            S.emit("dve", lambda e: e.scalar_tensor_tensor(out=rstd, in0=ssq[:, 0:n], scalar=1.0 / D, in1=nmr,
                                                           op0=ALU.mult, op1=ALU.subtract),
                   reads=["ps5", "T10"], writes=["T9"])
            S.emit("dve", lambda e: e.tensor_scalar(out=rstd, in0=rstd, scalar1=EPS / (ALPHA * ALPHA), scalar2=1e-30, op0=ALU.add, op1=ALU.max),
                   reads=["T9"], writes=["T9"])
            S.emit("act", lambda e: e.activation(out=rstd, in_=rstd, func=AF.Sqrt), reads=["T9"], writes=["T9"])
            S.emit("dve", lambda e: e.reciprocal(out=rstd, in_=rstd), reads=["T9"], writes=["T9"])
            S.emit("dve", lambda e: e.scalar_tensor_tensor(out=nmr, in0=mean, scalar=-1.0, in1=rstd, op0=ALU.mult, op1=ALU.mult),
                   reads=["T8", "T9"], writes=["T10"])
            for oc in range(16):
                eng = "dve" if oc % 2 == 0 else "pool"
                S.emit(eng, lambda e, oc=oc: e.tensor_tensor(out=R[:, oc, 0:n], in0=R[:, oc, 0:n], in1=rstd, op=ALU.mult),
                       reads=[("R", oc), "T9"], writes=[("R", oc)])
                S.emit(eng, lambda e, oc=oc: e.tensor_tensor(out=R[:, oc, 0:n], in0=R[:, oc, 0:n], in1=nmr, op=ALU.add),
                       reads=[("R", oc), "T10"], writes=[("R", oc)])
                S.emit("act", lambda e, oc=oc: e.activation(out=R[:, oc, 0:n], in_=R[:, oc, 0:n], func=AF.Identity,
                                                            scale=vcol(gname, oc), bias=vcol(bname, oc)),
                       reads=[("R", oc), "vecT"], writes=[("R", oc)])
                if ydst is not None:
                    S.emit("sp", lambda e, oc=oc: e.dma_start(out=ydst(oc), in_=R[:, oc, 0:n]), reads=[("R", oc)], dma=True)
                if next_l is not None:
                    modulate(next_l, R[:, oc, 0:n], ("R", oc), oc, nseq, t, col0)

        def layer1(nseq, t, col0, first_pass, last_pass, ydst, cc_dst, h_dst):
            n = nseq * t
            W = t + 3
            for blk in range(8):
                wxr, kxr = load_w(wo_in, blk * 256)
                ga_w, gx_w = GW[(2 * blk) % 4], GW[(2 * blk + 1) % 4]
                ga_k, gx_k = f"gw{(2 * blk) % 4}", f"gw{(2 * blk + 1) % 4}"
                S.emit("pool", lambda e, blk=blk, ga_w=ga_w: e.dma_start(out=ga_w[:], in_=wga[:, blk, :, :]), writes=[ga_k], dma=True)
                S.emit("pool", lambda e, blk=blk, gx_w=gx_w: e.dma_start(out=gx_w[:], in_=wgx[:, blk, :, :]), writes=[gx_k], dma=True)
                xcb = T[:, 0, :].bitcast(BF16)
                for m in range(2):
                    c = blk * 2 + m
                    pst, pk = proj_fm(wxr, kxr, m, n)
                    xv = xre[:, m, 0:nseq * W].rearrange("p (s w) -> p s w", s=nseq)
                    if first_pass is None:
                        pass
                    S.emit("pool", lambda e, xv=xv, c=c: e.tensor_copy(
                        out=xv[:, :, 0:3], in_=xrcar[:, c, 0:nseq * 3].rearrange("p (s w) -> p s w", s=nseq)),
                        reads=[("xrcar", c)], writes=[("xre", m)])
                    S.emit("act", lambda e, xv=xv, pst=pst: e.activation(
                        out=xv[:, :, 3:W], in_=pst[:, 0:n].rearrange("p (s t) -> p s t", s=nseq), func=AF.Copy),
                        reads=[pk], writes=[("xre", m)])
                    S.emit("pool", lambda e, xv=xv, c=c: e.tensor_copy(
                        out=xrcar[:, c, 0:nseq * 3].rearrange("p (s w) -> p s w", s=nseq), in_=xv[:, :, t:W]),
                        reads=[("xre", m)], writes=[("xrcar", c)])
                    if last_pass:
                        S.emit("sp", lambda e, c=c: e.dma_start(out=cc_dst(c), in_=xrcar[:, c, 0:nseq * 3]),
                               reads=[("xrcar", c)], dma=True)
                    xc = T[:, 1 + m, 0:n]
                    xc3 = xc.rearrange("p (s t) -> p s t", s=nseq)
                    S.emit("dve", lambda e, xv=xv, xc3=xc3, c=c: e.tensor_scalar(
                        out=xc3, in0=xv[:, :, 0:t], scalar1=vcol("wo_conv", 0 * 16 + c), scalar2=vcol("bo_conv", c),
                        op0=ALU.mult, op1=ALU.add), reads=[("xre", m), "vecT"], writes=[f"T{1 + m}"])
                    for j in range(1, 4):
                        S.emit("dve", lambda e, xv=xv, xc3=xc3, c=c, j=j: e.scalar_tensor_tensor(
                            out=xc3, in0=xv[:, :, j:j + t], scalar=vcol("wo_conv", j * 16 + c), in1=xc3,
                            op0=ALU.mult, op1=ALU.add), reads=[("xre", m), "vecT", f"T{1 + m}"], writes=[f"T{1 + m}"])
                    S.emit("act", lambda e, xc=xc, m=m: e.activation(out=xcb[:, m * 512:m * 512 + n], in_=xc, func=AF.Copy),
                           reads=[f"T{1 + m}"], writes=["T0" + "ab"[m]])
                wz, kz = load_w(wo_in, 2048 + blk * 256)
                for m in range(2):
                    pz, pzk = proj_fm(wz, kz, m, n)
                    S.emit("act", lambda e: e.activation(out=T[:, 8 + m, 0:n], in_=pz[:, 0:n], func=AF.Silu), reads=[pzk], writes=[f"T{8 + m}"])
                for m in range(2):
                    c = blk * 2 + m
                    xc = T[:, 1 + m, 0:n]
                    pa, pak = ps()
                    for kc in range(2):
                        S.emit("pe", lambda e, kc=kc, pa=pa, m=m: e.matmul(
                            pa[:, 0:n], ga_w[:, kc, m * 128:(m + 1) * 128], xcb[:, kc * 512:kc * 512 + n],
                            start=(kc == 0), stop=(kc == 1)), reads=[ga_k, "T0" + "ab"[kc]], writes=[pak])
                    px, pxk = ps()
                    for kc in range(2):
                        S.emit("pe", lambda e, kc=kc, px=px, m=m: e.matmul(
                            px[:, 0:n], gx_w[:, kc, m * 128:(m + 1) * 128], xcb[:, kc * 512:kc * 512 + n],
                            start=(kc == 0), stop=(kc == 1)), reads=[gx_k, "T0" + "ab"[kc]], writes=[pxk])
                    a_t = T[:, 3, 0:n]
                    b_t = T[:, 4, 0:n]
                    g_t = T[:, 5, 0:n]
                    h_t = T[:, 6, 0:n]
                    S.emit("act", lambda e: e.activation(out=a_t, in_=pa[:, 0:n], func=AF.Tanh,
                                                         bias=hv[:, 16 + c:17 + c], scale=0.5),
                           reads=[pak, "hv"], writes=["T3"])
                    S.emit("act", lambda e: e.activation(out=a_t, in_=a_t, func=AF.Exp, scale=hv[:, c:c + 1], bias=hv[:, c:c + 1]),
                           reads=["T3", "hv"], writes=["T3"])
                    S.emit("act", lambda e: e.activation(out=g_t, in_=px[:, 0:n], func=AF.Tanh,
                                                         bias=hv[:, 32 + c:33 + c], scale=0.5),
                           reads=[pxk, "hv"], writes=["T5"])
                    S.emit("dve", lambda e: e.scalar_tensor_tensor(out=b_t, in0=a_t, scalar=-1.0, in1=a_t, op0=ALU.mult, op1=ALU.mult),
                           reads=["T3"], writes=["T4"])
                    S.emit("dve", lambda e: e.tensor_scalar(out=b_t, in0=b_t, scalar1=1.0, scalar2=1e-30, op0=ALU.add, op1=ALU.max),
                           reads=["T4"], writes=["T4"])
                    S.emit("act", lambda e: e.activation(out=b_t, in_=b_t, func=AF.Sqrt), reads=["T4"], writes=["T4"])
                    S.emit("dve", lambda e: e.scalar_tensor_tensor(out=g_t, in0=g_t, scalar=1.0, in1=xc, op0=ALU.add, op1=ALU.mult),
                           reads=["T5", f"T{1 + m}"], writes=["T5"])
                    S.emit("dve", lambda e: e.scalar_tensor_tensor(out=b_t, in0=b_t, scalar=0.5, in1=g_t, op0=ALU.mult, op1=ALU.mult),
                           reads=["T4", "T5"], writes=["T4"])
                    if nseq > 1:
                        a3 = a_t.rearrange("p (s t) -> p s t", s=nseq)
                        b3 = b_t.rearrange("p (s t) -> p s t", s=nseq)
                        h0 = T[:, 7, 0:nseq]
                        S.emit("sp", lambda e, c=c: e.dma_start(out=h0, in_=fm(shT)[:, c, :]), writes=["T7"], dma=True)
                        S.emit("dve", lambda e, a3=a3, h0=h0: e.tensor_tensor(out=h0, in0=h0, in1=a3[:, :, 0], op=ALU.mult),
                               reads=["T7", "T3"], writes=["T7"])
                        S.emit("dve", lambda e, b3=b3, h0=h0: e.tensor_tensor(out=b3[:, :, 0], in0=b3[:, :, 0], in1=h0, op=ALU.add),
                               reads=["T7", "T4"], writes=["T4"])
                        S.emit("dve", lambda e, a3=a3: e.memset(a3[:, :, 0], 0.0), reads=["T3"], writes=["T3"])
                        init = 0.0
                        ireads = []
                    else:
                        init = 0.0 if first_pass else hcar[:, c:c + 1]
                        ireads = [] if first_pass else [("hcar", c)]
                    S.emit("dve", lambda e, init=init: e.tensor_tensor_scan(out=h_t, data0=a_t, data1=b_t, initial=init,
                                                                            op0=ALU.mult, op1=ALU.add),
                           reads=["T3", "T4"] + ireads, writes=["T6"])
                    if nseq == 1:
                        S.emit("pool", lambda e, c=c: e.tensor_copy(out=hcar[:, c:c + 1], in_=h_t[:, n - 1:n]),
                               reads=["T6"], writes=[("hcar", c)])
                        if last_pass:
                            S.emit("sp", lambda e, c=c: e.dma_start(out=h_dst(c), in_=hcar[:, c:c + 1]), reads=[("hcar", c)], dma=True)
                    else:
                        h3 = h_t.rearrange("p (s t) -> p s t", s=nseq)
                        S.emit("pool", lambda e, h3=h3: e.tensor_copy(out=T[:, 7, 64:64 + nseq], in_=h3[:, :, t - 1]),
                               reads=["T6"], writes=["T7"])
                        S.emit("sp", lambda e, c=c: e.dma_start(out=h_dst(c), in_=T[:, 7, 64:64 + nseq]), reads=["T7"], dma=True)
                    S.emit("dve", lambda e: e.tensor_tensor(out=mixT[:, c, 0:n], in0=h_t, in1=T[:, 8 + m, 0:n], op=ALU.mult),
                           reads=["T6", f"T{8 + m}"], writes=[("mixT", c)])
            outproj_ln(1, wo_out, "go_ln", "bo_ln", nseq, t, col0,
                       lambda oc: (R[:, oc, 0:n], ("R", oc)), ydst, True, None)

        def convb_branch(nseq, t, first_pass, last_pass, cb_dst):
            n = nseq * t
            W = t + 2
            for pr in range(4):
                wcg, kcg = load_w(we_in, 5120 + pr * 256)
                for m in range(2):
                    pst, pk = proj_fm(wcg, kcg, m, n)
                    S.emit("act", lambda e, pst=pst, m=m: e.activation(out=T[:, 1 + m, 0:n], in_=pst[:, 0:n], func=AF.Copy),
                           reads=[pk], writes=[f"T{1 + m}"])
                bg()
                wxi, kxi = load_w(we_in, 6144 + pr * 256)
                for m in range(2):
                    c = pr * 2 + m
                    pst, pk = proj_fm(wxi, kxi, m, n)
                    xv = cxe[:, m, 0:nseq * W].rearrange("p (s w) -> p s w", s=nseq)
                    S.emit("pool", lambda e, xv=xv, c=c: e.tensor_copy(
                        out=xv[:, :, 0:2], in_=cxcar[:, c, 0:nseq * 2].rearrange("p (s w) -> p s w", s=nseq)),
                        reads=[("cxcar", c)], writes=[("cxe", m)])
                    S.emit("dve", lambda e, xv=xv, pst=pst, m=m: e.tensor_tensor(
                        out=xv[:, :, 2:W], in0=pst[:, 0:n].rearrange("p (s t) -> p s t", s=nseq),
                        in1=T[:, 1 + m, 0:n].rearrange("p (s t) -> p s t", s=nseq), op=ALU.mult),
                        reads=[pk, f"T{1 + m}"], writes=[("cxe", m)])
                    S.emit("pool", lambda e, xv=xv, c=c: e.tensor_copy(
                        out=cxcar[:, c, 0:nseq * 2].rearrange("p (s w) -> p s w", s=nseq), in_=xv[:, :, t:W]),
                        reads=[("cxe", m)], writes=[("cxcar", c)])
                    if last_pass:
                        S.emit("sp", lambda e, c=c: e.dma_start(out=cb_dst(c), in_=cxcar[:, c, 0:nseq * 2]),
                               reads=[("cxcar", c)], dma=True)
                    cv = T[:, 3 + m, 0:n]
                    cv3 = cv.rearrange("p (s t) -> p s t", s=nseq)
                    S.emit("dve", lambda e, xv=xv, cv3=cv3, c=c: e.tensor_scalar(
                        out=cv3, in0=xv[:, :, 0:t], scalar1=vcol("we_conv", 0 * 8 + c), scalar2=None, op0=ALU.mult),
                        reads=[("cxe", m), "vecT"], writes=[f"T{3 + m}"])
                    for j in range(1, 3):
                        S.emit("dve", lambda e, xv=xv, cv3=cv3, c=c, j=j: e.scalar_tensor_tensor(
                            out=cv3, in0=xv[:, :, j:j + t], scalar=vcol("we_conv", j * 8 + c), in1=cv3,
                            op0=ALU.mult, op1=ALU.add), reads=[("cxe", m), "vecT", f"T{3 + m}"], writes=[f"T{3 + m}"])
                bg()
                wbg, kbg = load_w(we_in, 4096 + pr * 256)
                for m in range(2):
                    cv = T[:, 3 + m, 0:n]
                    pst, pk = proj_fm(wbg, kbg, m, n)
                    S.emit("dve", lambda e, pst=pst, cv=cv: e.tensor_tensor(out=cv, in0=pst[:, 0:n], in1=cv, op=ALU.mult),
                           reads=[pk, f"T{3 + m}"], writes=[f"T{3 + m}"])
                bg()
                wzb, kzb = load_w(we_in, 7168 + pr * 256)
                for m in range(2):
                    c = pr * 2 + m
                    cv = T[:, 3 + m, 0:n]
                    pst2, pk2 = proj_fm(wzb, kzb, m, n)
                    S.emit("act", lambda e, pst2=pst2, m=m: e.activation(out=T[:, 5 + m, 0:n], in_=pst2[:, 0:n], func=AF.Silu),
                           reads=[pk2], writes=[f"T{5 + m}"])
                    S.emit("dve", lambda e, cv=cv, m=m, c=c: e.tensor_tensor(out=mixT[:, 8 + c, 0:n], in0=cv, in1=T[:, 5 + m, 0:n], op=ALU.mult),
                           reads=[f"T{3 + m}", f"T{5 + m}"], writes=[("mixT", 8 + c)])

        def gate(k):
            if stage < k:
                raise _Stop()

        try:
            if do_prompt:
              gate(1)
              S.emit("dve", lambda e: e.memset(cxcar[:], 0.0), writes=[("cxcar", c) for c in range(8)])
              S.emit("dve", lambda e: e.memset(xrcar[:], 0.0), writes=[("xrcar", c) for c in range(16)])
              for p in range(npass):
                  tok0 = p * NT
                  first, last = (p == 0), (p == npass - 1)
                  for c in range(16):
                      xs_, xk = T[:, c % 4, :], f"T{c % 4}"
                      S.emit("sp", lambda e, c=c, xs_=xs_: e.dma_start(out=xs_, in_=fm(xpT)[:, c, tok0:tok0 + NT]), writes=[xk], dma=True)
                      modulate(0, xs_, xk, c, 1, NT, 0)
                  gate(1.2)
                  ps_banks[0] = [6, 7]
                  nkb = (tok0 + NT) // 128
                  for hp in range(4):
                      wk, kk = load_w(we_in, 1024 + hp * 256)
                      for m in range(2):
                          h = hp * 2 + m
                          pst, pk = proj_fm(wk, kk, m, NT)
                          S.emit("act", lambda e, pst=pst, h=h: e.activation(out=KT[:, h, tok0:tok0 + NT], in_=pst[:, 0:NT], func=AF.Copy),
                                 reads=[pk], writes=[("KT", h)])
                          S.emit("dve", lambda e, pst=pst, m=m: e.tensor_copy(out=T[:, 8 + m, :], in_=pst[:, 0:NT]),
                                 reads=[pk], writes=[f"T{8 + m}"])
                          S.emit("sp", lambda e, h=h, m=m: e.dma_start(out=kpT[h * 128:(h + 1) * 128, tok0:tok0 + NT], in_=T[:, 8 + m, :]),
                                 reads=[f"T{8 + m}"], dma=True)
                      gate(1.4)
                      bg()
                      wv, kv = load_w(we_in, 2048 + hp * 256)
                      for tt in range(4):
                          pst, pk = ps()
                          for kc in range(16):
                              S.emit("pe", lambda e, kc=kc, pst=pst, tt=tt: e.matmul(
                                  pst[:, 0:256], uT[:, kc, tt * 128:(tt + 1) * 128], wv[:, kc, :],
                                  start=(kc == 0), stop=(kc == 15)), reads=[kv, ("uT", kc)], writes=[pk])
                          kb = tok0 // 128 + tt
                          S.emit("act", lambda e, pst=pst, kb=kb: e.activation(out=V[:, kb, hp * 256:(hp + 1) * 256], in_=pst[:, 0:256], func=AF.Copy),
                                 reads=[pk], writes=[("V", hp)])
                          vst, vk = T[:, 10, tt * 128:tt * 128 + 128], "T10"
                          vst = T[:, 10, 0:256] if tt % 2 == 0 else T[:, 10, 256:512]
                          S.emit("dve", lambda e, pst=pst, vst=vst: e.tensor_copy(out=vst, in_=pst[:, 0:256]), reads=[pk], writes=["T10" + "ab"[tt % 2]])
                          S.emit("sp", lambda e, kb=kb, vst=vst: e.dma_start(out=vp[kb * 128:(kb + 1) * 128, hp * 256:(hp + 1) * 256], in_=vst),
                                 reads=["T10" + "ab"[tt % 2]], dma=True)
                      gate(2)
                      bg()
                      qT = T[:, 4, :].bitcast(BF16)
                      zaT = T[:, 5, :].bitcast(BF16)
                      wq, kq = load_w(we_in, hp * 256)
                      for m in range(2):
                          pst, pk = proj_fm(wq, kq, m, NT)
                          S.emit("act", lambda e, pst=pst, m=m: e.activation(out=qT[:, m * 512:(m + 1) * 512], in_=pst[:, 0:NT], func=AF.Copy),
                                 reads=[pk], writes=["T4" + "ab"[m]])
                      bg()
                      wza, kza = load_w(we_in, 3072 + hp * 256)
                      for m in range(2):
                          pst2, pk2 = proj_fm(wza, kza, m, NT)
                          S.emit("act", lambda e, pst2=pst2, m=m: e.activation(out=zaT[:, m * 512:(m + 1) * 512], in_=pst2[:, 0:NT], func=AF.Silu),
                                 reads=[pk2], writes=["T5" + "ab"[m]])
                      lsf = T[:, 6, :]
                      lsb = T[:, 7, :].bitcast(BF16)[:, 0:512]
                      et = T[:, 9, :]
                      tmp = T[:, 10, :]
                      items = [(m, bi, kb) for m in range(2) for bi, kb in enumerate(range(nkb - 1, -1, -1))]

                      def bufs(g):
                          sps, spk = PSt[:, g % 2, :], f"ps{g % 2}"
                          cps, cpk = PSt[:, 2 + g % 2, :], f"ps{2 + g % 2}"
                          spb = T[:, 8, :].bitcast(BF16)[:, (g % 2) * 512:(g % 2) * 512 + 512]
                          spk2 = "T8" + "ab"[g % 2]
                          wbf = T[:, 11, :].bitcast(BF16)[:, (g % 2) * 512:(g % 2) * 512 + 512]
                          wkk = "T11" + "ab"[g % 2]
                          return sps, spk, cps, cpk, spb, spk2, wbf, wkk

                      def stage_a0(g):
                          m, bi, kb = items[g]
                          h = hp * 2 + m
                          sps, spk, cps, cpk, spb, spk2, wbf, wkk = bufs(g)
                          S.emit("pe", lambda e: e.matmul(sps, KT[:, h, kb * 128:(kb + 1) * 128], qT[:, m * 512:(m + 1) * 512],
                                                          start=True, stop=True),
                                 reads=[("KT", h), "T4" + "ab"[m]], writes=[spk])

                      def stage_a1(g):
                          m, bi, kb = items[g]
                          h = hp * 2 + m
                          o = kb * 128 - tok0
                          sps, spk, cps, cpk, spb, spk2, wbf, wkk = bufs(g)
                          S.emit("act", lambda e: e.activation(out=et, in_=sps, func=AF.Exp, scale=QSCALE, bias=vcol("sbb", h)),
                                 reads=[spk, "vecT"], writes=["T9"])
                          S.emit("act", lambda e: e.activation(out=spb, in_=et, func=AF.Ln, bias=one1[:, 0:1], scale=1.0),
                                 reads=["T9", "one1"], writes=[spk2])
                          if o >= 0:
                              S.emit("pool", lambda e: e.tensor_tensor(out=spb, in0=spb, in1=maskp[:, o // 128, :], op=ALU.mult),
                                     reads=[spk2, "maskp"], writes=[spk2])
                          S.emit("pe", lambda e: e.matmul(cps, tri, spb, start=True, stop=(bi == 0)),
                                 reads=[spk2, "tri3"], writes=[cpk])
                          if bi > 0:
                              S.emit("pe", lambda e: e.matmul(cps, ones, lsb, start=False, stop=True),
                                     reads=["T7a", "tri3"], writes=[cpk])
                          if bi == 0:
                              S.emit("pool", lambda e: e.tensor_copy(out=lsf, in_=spb), reads=[spk2], writes=["T6"])
                          else:
                              S.emit("pool", lambda e: e.tensor_tensor(out=lsf, in0=lsf, in1=spb, op=ALU.add),
                                     reads=[spk2, "T6"], writes=["T6"])

                      def stage_a2(g):
                          m, bi, kb = items[g]
                          if bi < nkb - 1:
                              S.emit("dve", lambda e: e.tensor_copy(out=lsb, in_=lsf), reads=["T6"], writes=["T7a"])

                      def stage_b1(g):
                          sps, spk, cps, cpk, spb, spk2, wbf, wkk = bufs(g)
                          S.emit("dve", lambda e: e.tensor_tensor(out=tmp, in0=cps, in1=spb, op=ALU.add),
                                 reads=[cpk, spk2], writes=["T10"])
                          S.emit("dve", lambda e: e.scalar_tensor_tensor(out=tmp, in0=sps, scalar=QSCALE, in1=tmp,
                                                                         op0=ALU.mult, op1=ALU.subtract),
                                 reads=[spk, "T10"], writes=["T10"])

                      def stage_b2(g):
                          m, bi, kb = items[g]
                          h = hp * 2 + m
                          o = kb * 128 - tok0
                          oacc, oak = PSt[:, 4 + m, :], f"ps{4 + m}"
                          sps, spk, cps, cpk, spb, spk2, wbf, wkk = bufs(g)
                          S.emit("act", lambda e: e.activation(out=wbf, in_=tmp, func=AF.Exp, scale=1.0, bias=vcol("sbb", h)),
                                 reads=["T10", "vecT"], writes=[wkk])
                          if o >= 0:
                              S.emit("dve", lambda e: e.tensor_tensor(out=wbf, in0=wbf, in1=maskp[:, o // 128, :], op=ALU.mult),
                                     reads=[wkk, "maskp"], writes=[wkk])
                          S.emit("pe", lambda e: e.matmul(oacc, V[:, kb, h * 128:(h + 1) * 128], wbf, start=(bi == 0), stop=(bi == nkb - 1)),
                                 reads=[wkk, ("V", h // 2)], writes=[oak])
                          if bi == nkb - 1:
                              S.emit("dve", lambda e: e.tensor_tensor(out=mixT[:, h, :], in0=oacc, in1=zaT[:, m * 512:(m + 1) * 512], op=ALU.mult),
                                     reads=[oak, "T5" + "ab"[m]], writes=[("mixT", h)])

                      NI = len(items)
                      stage_a0(0)
                      stage_a0(1)
                      stage_a1(0)
                      stage_a2(0)
                      for g in range(NI):
                          stage_b1(g)
                          if g + 2 < NI:
                              stage_a0(g + 2)
                          if g + 1 < NI:
                              stage_a1(g + 1)
                          stage_b2(g)
                          if g + 1 < NI:
                              stage_a2(g + 1)
                  gate(4)
                  ps_banks[0] = [0, 1, 2, 3, 6, 7]
                  convb_branch(1, NT, first, last, lambda c: cbpT[c * 128:(c + 1) * 128, :])

                  gate(5)
                  bg(24)
                  def resid0(oc, tok0=tok0):
                      xs_, xk = T[:, oc % 4, :], f"T{oc % 4}"
                      S.emit("sp", lambda e: e.dma_start(out=xs_, in_=fm(xpT)[:, oc, tok0:tok0 + NT]), writes=[xk], dma=True)
                      return xs_, xk
                  outproj_ln(0, we_out, "ge_ln", "be_ln", 1, NT, 0, resid0, None, True, 1)
                  gate(6)
                  layer1(1, NT, 0, first, last,
                         lambda oc, tok0=tok0: ypT[oc * 128:(oc + 1) * 128, tok0:tok0 + NT],
                         lambda c: ccpT[c * 128:(c + 1) * 128, :],
                         lambda c: hpT[c * 128:(c + 1) * 128, :])


            if do_sample:
                NS, TS, NSM = 16, 8, 128
                gate(7)
                S.emit("sp", lambda e: e.dma_start(out=cxcar[:].rearrange("p c (s w) -> p c s w", s=16), in_=fm(scbT)),
                       writes=[("cxcar", c) for c in range(8)], dma=True)
                S.emit("sp", lambda e: e.dma_start(out=xrcar[:].rearrange("p c (s w) -> p c s w", s=16), in_=fm(sccT)),
                       writes=[("xrcar", c) for c in range(16)], dma=True)
                pti = T[:, 8, 0:256].bitcast(I32)
                ptf = T[:, 9, 0:256]
                S.emit("sp", lambda e: e.dma_start(out=pti, in_=ptab[0, :].partition_broadcast(128)), writes=["T8"], dma=True)
                S.emit("dve", lambda e: e.tensor_copy(out=ptf, in_=pti), reads=["T8"], writes=["T9"])
                S.emit("dve", lambda e: e.tensor_scalar(out=ptf, in0=ptf, scalar1=128.0, scalar2=iota[:, 0:1], op0=ALU.mult, op1=ALU.add),
                       reads=["T9", "iota"], writes=["T9"])
                S.emit("dve", lambda e: e.tensor_copy(out=idx[:], in_=ptf), reads=["T9"], writes=["idx"])
                for c in range(16):
                    xs_, xk = T[:, c % 4, 0:NSM], f"T{c % 4}"
                    S.emit("sp", lambda e: e.dma_start(out=xs_, in_=fm(xsT)[:, c, :]), writes=[xk], dma=True)
                    modulate(0, xs_, xk, c, NS, TS, 1)
                knew = R[:, 0, :].bitcast(BF16)
                vnew = R[:, 1, :].bitcast(BF16)
                qs = R[:, 2, :].bitcast(BF16)
                zas = R[:, 3, :].bitcast(BF16)
                for hp in range(4):
                    wk, kk = load_w(we_in, 1024 + hp * 256)
                    for m in range(2):
                        h = hp * 2 + m
                        pst, pk = proj_fm(wk, kk, m, NSM)
                        S.emit("act", lambda e: e.activation(out=knew[:, h * 128:(h + 1) * 128], in_=pst[:, 0:NSM], func=AF.Copy),
                               reads=[pk], writes=[("R", 0)])
                        S.emit("dve", lambda e: e.tensor_copy(out=T[:, 8 + m, 0:NSM], in_=pst[:, 0:NSM]), reads=[pk], writes=[f"T{8 + m}"])
                        S.emit("sp", lambda e: e.dma_start(out=ksT[h * 128:(h + 1) * 128, :], in_=T[:, 8 + m, 0:NSM]),
                               reads=[f"T{8 + m}"], dma=True)
                    wv, kv = load_w(we_in, 2048 + hp * 256)
                    pst, pk = ps()
                    for kc in range(16):
                        S.emit("pe", lambda e: e.matmul(pst[:, 0:256], uT[:, kc, 0:NSM], wv[:, kc, :], start=(kc == 0), stop=(kc == 15)),
                               reads=[kv, ("uT", kc)], writes=[pk])
                    S.emit("act", lambda e: e.activation(out=vnew[:, hp * 256:(hp + 1) * 256], in_=pst[:, 0:256], func=AF.Copy),
                           reads=[pk], writes=[("R", 1)])
                    S.emit("dve", lambda e: e.tensor_copy(out=T[:, 10, 0:256], in_=pst[:, 0:256]), reads=[pk], writes=["T10"])
                    S.emit("sp", lambda e: e.dma_start(out=vs[:, hp * 256:(hp + 1) * 256], in_=T[:, 10, 0:256]), reads=["T10"], dma=True)
                    wq, kq = load_w(we_in, hp * 256)
                    for m in range(2):
                        h = hp * 2 + m
                        pst, pk = proj_fm(wq, kq, m, NSM)
                        S.emit("act", lambda e: e.activation(out=qs[:, h * 128:(h + 1) * 128], in_=pst[:, 0:NSM], func=AF.Copy),
                               reads=[pk], writes=[("R", 2)])
                    wza, kza = load_w(we_in, 3072 + hp * 256)
                    for m in range(2):
                        h = hp * 2 + m
                        pst2, pk2 = proj_fm(wza, kza, m, NSM)
                        S.emit("act", lambda e: e.activation(out=zas[:, h * 128:(h + 1) * 128], in_=pst2[:, 0:NSM], func=AF.Silu),
                               reads=[pk2], writes=[("R", 3)])
                gate(8)
                ps_banks[0] = [6, 7]
                NCOL = 17 * 64
                zps = PSt[:, 0:3, :].rearrange("p a b -> p (a b)")
                cps = PSt[:, 3:6, :].rearrange("p a b -> p (a b)")
                Tf = T[:].rearrange("p a b -> p (a b)")
                et = Tf[:, 0:NCOL]
                lsf = Tf[:, 3 * 512:3 * 512 + NCOL]
                spb = Tf[:, 6 * 512:8 * 512].bitcast(BF16)[:, 0:NCOL]
                Rf = R[:].rearrange("p a b -> p (a b)")
                wbf = Rf[:, 12 * 512:14 * 512].bitcast(BF16)[:, 0:NCOL]
                lsb = Rf[:, 14 * 512:16 * 512].bitcast(BF16)[:, 0:NCOL]
                K_ET, K_LSF, K_SPB = ["T0", "T1", "T2"], ["T3", "T4", "T5"], ["T6", "T7"]
                K_WBF, K_LSB = [("R", 12), ("R", 13)], [("R", 14), ("R", 15)]
                K_ZPS, K_CPS = ["ps0", "ps1", "ps2"], ["ps3", "ps4", "ps5"]
                qs3 = qs.rearrange("p (h t) -> p h t", h=8)
                zas3 = zas.rearrange("p (h t) -> p h t", h=8)
                def part_k(s_):
                    ksts = []
                    for pg in range(16):
                        slot = 4 + pg % 8
                        kst = R[:, slot, :].bitcast(BF16)
                        S.emit("pool", lambda e: e.indirect_dma_start(
                            out=kst, out_offset=None, in_=cache_k,
                            in_offset=bass.IndirectOffsetOnAxis(ap=idx[:, s_ * 16 + pg:s_ * 16 + pg + 1], axis=0)),
                            reads=["idx"], writes=[("R", slot)], dma=True)
                        bank = 6 + pg % 2
                        ptb = PSt[:, bank, :].bitcast(BF16)
                        for h in range(8):
                            S.emit("pe", lambda e: e.transpose(ptb[:, h * 128:(h + 1) * 128], kst[:, h * 128:(h + 1) * 128], ident),
                                   reads=[("R", slot), "tri3"], writes=[f"ps{bank}"])
                        ev = "act" if pg % 2 == 0 else "dve"
                        if ev == "act":
                            S.emit("act", lambda e: e.activation(out=KT[:, :, pg * 128:(pg + 1) * 128],
                                                                 in_=ptb.rearrange("p (h k) -> p h k", h=8), func=AF.Copy),
                                   reads=[f"ps{bank}"], writes=[("KT", h) for h in range(8)])
                        else:
                            S.emit("dve", lambda e: e.tensor_copy(out=KT[:, :, pg * 128:(pg + 1) * 128],
                                                                  in_=ptb.rearrange("p (h k) -> p h k", h=8)),
                                   reads=[f"ps{bank}"], writes=[("KT", h) for h in range(8)])

                def part_v(s_):
                    for pg in range(16):
                        S.emit("pool", lambda e: e.indirect_dma_start(
                            out=V[:, pg, :], out_offset=None, in_=cache_v,
                            in_offset=bass.IndirectOffsetOnAxis(ap=idx[:, s_ * 16 + pg:s_ * 16 + pg + 1], axis=0)),
                            reads=["idx"], writes=[("Vp", pg)], dma=True)

                def part_s(s_):
                    for pg in range(17):
                        for h in range(8):
                            lhs = KT[:, h, pg * 128:(pg + 1) * 128] if pg < 16 else knew[:, h * 128:(h + 1) * 128]
                            rk = [("KT", h)] if pg < 16 else [("R", 0)]
                            c0 = pg * 64 + h * 8
                            S.emit("pe", lambda e: e.matmul(zps[:, c0:c0 + 8], lhs, qs3[:, h, s_ * 8:(s_ + 1) * 8], start=True, stop=True),
                                   reads=rk + [("R", 2)], writes=[f"ps{c0 // 512}"])
                    zps4 = zps[:, 0:NCOL].rearrange("p (g h q) -> p g h q", g=17, h=8)
                    et4 = et.rearrange("p (g h q) -> p g h q", g=17, h=8)
                    for h in range(8):
                        S.emit("act", lambda e: e.activation(out=et4[:, :, h, :], in_=zps4[:, :, h, :], func=AF.Exp,
                                                             scale=QSCALE, bias=vcol("sbb", h)),
                               reads=K_ZPS + ["vecT"], writes=K_ET)
                    S.emit("act", lambda e: e.activation(out=spb, in_=et, func=AF.Ln, bias=one1[:, 0:1], scale=1.0),
                           reads=K_ET + ["one1"], writes=K_SPB)
                    mk = masknew[:, s_, :].unsqueeze(1).to_broadcast([128, 8, 8])
                    spn = spb[:, 1024:NCOL].rearrange("p (h q) -> p h q", h=8)
                    S.emit("dve", lambda e: e.tensor_tensor(out=spn, in0=spn, in1=mk, op=ALU.mult),
                           reads=K_SPB + ["masknew"], writes=K_SPB)

                def part_e(s_):
                    mk = masknew[:, s_, :].unsqueeze(1).to_broadcast([128, 8, 8])
                    S.emit("dve", lambda e: e.memset(lsf[:, 1024:NCOL], 0.0), writes=K_LSF)
                    for pg in range(15, -1, -1):
                        S.emit("dve", lambda e: e.tensor_tensor(out=lsf[:, pg * 64:(pg + 1) * 64], in0=lsf[:, (pg + 1) * 64:(pg + 2) * 64],
                                                                in1=spb[:, (pg + 1) * 64:(pg + 2) * 64], op=ALU.add),
                               reads=K_LSF + K_SPB, writes=K_LSF)
                    S.emit("act", lambda e: e.activation(out=lsb, in_=lsf, func=AF.Copy), reads=K_LSF, writes=K_LSB)
                    for (c0, c1) in [(0, 512), (512, 1024), (1024, NCOL)]:
                        S.emit("pe", lambda e: e.matmul(cps[:, c0:c1], tri, spb[:, c0:c1], start=True, stop=False),
                               reads=K_SPB + ["tri3"], writes=[f"ps{3 + c0 // 512}"])
                        S.emit("pe", lambda e: e.matmul(cps[:, c0:c1], ones, lsb[:, c0:c1], start=False, stop=True),
                               reads=K_LSB + ["tri3"], writes=[f"ps{3 + c0 // 512}"])
                    S.emit("dve", lambda e: e.tensor_tensor(out=lsf, in0=cps[:, 0:NCOL], in1=spb, op=ALU.add),
                           reads=K_CPS + K_SPB, writes=K_LSF)
                    S.emit("act", lambda e: e.activation(out=lsf, in_=lsf, func=AF.Exp, scale=-1.0), reads=K_LSF, writes=K_LSF)
                    S.emit("dve", lambda e: e.tensor_tensor(out=wbf, in0=et, in1=lsf, op=ALU.mult), reads=K_ET + K_LSF, writes=K_WBF)
                    wn = wbf[:, 1024:NCOL].rearrange("p (h q) -> p h q", h=8)
                    S.emit("dve", lambda e: e.tensor_tensor(out=wn, in0=wn, in1=mk, op=ALU.mult), reads=K_WBF + ["masknew"], writes=K_WBF)
                    ops_, opk = PSt[:, 7, :], "ps7"
                    for h in range(8):
                        for pg in range(17):
                            lhs = V[:, pg, h * 128:(h + 1) * 128] if pg < 16 else vnew[:, h * 128:(h + 1) * 128]
                            rk = [("Vp", pg)] if pg < 16 else [("R", 1)]
                            c0 = pg * 64 + h * 8
                            S.emit("pe", lambda e: e.matmul(ops_[:, h * 8:(h + 1) * 8], lhs, wbf[:, c0:c0 + 8], start=(pg == 0), stop=(pg == 16)),
                                   reads=rk + K_WBF, writes=[opk])
                    S.emit("dve", lambda e: e.tensor_tensor(out=mixT[:, 0:8, s_ * 8:(s_ + 1) * 8],
                                                            in0=ops_[:, 0:64].rearrange("p (h q) -> p h q", h=8),
                                                            in1=zas3[:, :, s_ * 8:(s_ + 1) * 8], op=ALU.mult),
                           reads=[opk, ("R", 3)], writes=[("mixT", h) for h in range(8)])

                part_k(0)
                part_s(0)
                part_v(0)
                for s_ in range(NS):
                    if s_ + 1 < NS:
                        part_k(s_ + 1)
                    part_e(s_)
                    if s_ + 1 < NS:
                        part_s(s_ + 1)
                        part_v(s_ + 1)
                gate(9)
                ps_banks[0] = [0, 1, 2, 3, 6, 7]
                convb_branch(NS, TS, False, True, lambda c: cbsT[c * 128:(c + 1) * 128, :, :].rearrange("p s w -> p (s w)"))

                def resid_s(oc):
                    xs_, xk = T[:, oc % 4, 0:NSM], f"T{oc % 4}"
                    S.emit("sp", lambda e: e.dma_start(out=xs_, in_=fm(xsT)[:, oc, :]), writes=[xk], dma=True)
                    return xs_, xk
                outproj_ln(0, we_out, "ge_ln", "be_ln", NS, TS, 1, resid_s, None, True, 1)
                gate(10)
                layer1(NS, TS, 1, None, True,
                       lambda oc: ysT[oc * 128:(oc + 1) * 128, :],
                       lambda c: ccsT[c * 128:(c + 1) * 128, :, :].rearrange("p s w -> p (s w)"),
                       lambda c: hsT[c * 128:(c + 1) * 128, :])
        except _Stop:
            pass
        S.finish("sp")
        S.replay()
    nc._in_names = in_names
    if wseq is None:
        return build(do_prompt, do_sample, npass, stage, wseq=wrec)
    return nc


_cache = {}


def _get_nc():
    if "nc" not in _cache:
        _cache["nc"] = build()
    return _cache["nc"]


def _w_layout(w):
    n = w.shape[1]
    return np.ascontiguousarray(w.reshape(16, 128, n).transpose(1, 0, 2))


def _colv(v):
    return np.asarray(v, dtype=np.float32).reshape(-1, 128).T


def prep_inputs(inp):
    f = lambda a: np.asarray(a, dtype=np.float32)
    shared = {}
    shared["cache_k"] = f(inp["cache_k"]).reshape(NPHYS * 128, 1024)
    shared["cache_v"] = f(inp["cache_v"]).reshape(NPHYS * 128, 1024)
    shared["we_ada"] = _w_layout(f(inp["we_ada"])[0])
    shared["we_in"] = _w_layout(f(inp["we_in"])[0])
    shared["we_out"] = _w_layout(f(inp["we_out"])[0])
    shared["wo_ada"] = _w_layout(f(inp["wo_ada"])[0])
    shared["wo_in"] = _w_layout(f(inp["wo_in"])[0])
    shared["wo_out"] = _w_layout(f(inp["wo_out"])[0])
    shared["wga"] = np.ascontiguousarray(f(inp["wo_gate_a"])[0].reshape(8, 2, 128, 256).transpose(2, 0, 1, 3))
    shared["wgx"] = np.ascontiguousarray(f(inp["wo_gate_x"])[0].reshape(8, 2, 128, 256).transpose(2, 0, 1, 3))
    vec = [_colv(f(inp["be_ada"])[0]), _colv(f(inp["bo_ada"])[0]), _colv(f(inp["ge_ln"])[0]), _colv(f(inp["be_ln"])[0]),
           _colv(f(inp["go_ln"])[0]), _colv(f(inp["bo_ln"])[0]), _colv(f(inp["we_conv"])[0].reshape(-1)),
           _colv(f(inp["wo_conv"])[0].reshape(-1)), _colv(f(inp["bo_conv"])[0]), _colv(f(inp["bo_gate_a"])[0]),
           _colv(f(inp["bo_gate_x"])[0]), _colv(f(inp["wo_lambda"])[0]),
           np.broadcast_to(f(inp["we_sb_bias"])[0][None, :], (128, 8))]
    shared["vecT"] = np.ascontiguousarray(np.concatenate(vec, axis=1))
    assert shared["vecT"].shape == (128, NV)
    j = np.arange(128)
    tri = (j[:, None] > j[None, :]).astype(np.float32)
    ones = np.ones((128, 128), np.float32)
    ident = np.eye(128, dtype=np.float32)
    shared["c_tri"] = np.ascontiguousarray(np.stack([tri, ones, ident], axis=1))
    tq = np.arange(512)
    shared["c_maskp"] = np.ascontiguousarray(np.stack(
        [(tq[None, :] > (j[:, None] + 128 * o)).astype(np.float32) for o in range(4)], axis=1))
    s_ = np.arange(16)
    q_ = np.arange(8)
    shared["c_masknew"] = np.ascontiguousarray(
        ((j[:, None, None] // 8 == s_[None, :, None]) & (j[:, None, None] % 8 < q_[None, None, :])).astype(np.float32))
    shared["c_iota"] = j.astype(np.float32).reshape(128, 1)
    xp = f(inp["x_prompt"])
    xs = f(inp["x_sample"])
    cp = f(inp["c_prompt"])
    cs = f(inp["c_sample"])
    pt = np.asarray(inp["page_table"], dtype=np.int32)
    scb = f(inp["state_conv_b"])[0]
    scc = f(inp["state_conv_c"])[0]
    sh = f(inp["state_h"])[0]
    maps = []
    for c in range(NCORES):
        b = c % 4
        sl = slice(c * 16, (c + 1) * 16)
        m = dict(shared)
        m["xpT"] = np.ascontiguousarray(xp[b].T)
        m["xsT"] = np.ascontiguousarray(xs[sl].reshape(128, D).T)
        m["cT"] = np.ascontiguousarray(np.concatenate([cp[b:b + 1], cs[sl]], axis=0).T)
        m["ptab"] = np.ascontiguousarray(pt[sl].reshape(1, 256))
        m["scbT"] = np.ascontiguousarray(scb[sl].transpose(2, 0, 1))
        m["sccT"] = np.ascontiguousarray(scc[sl].transpose(2, 0, 1))
        m["shT"] = np.ascontiguousarray(sh[sl].T)
        maps.append(m)
    return maps


def assemble(results):
    B, SEQ = 4, 2048
    yp = np.stack([results[b]["ypT"].T for b in range(B)])
    ys = np.concatenate([results[c]["ysT"].T.reshape(16, 8, D) for c in range(NCORES)], axis=0)
    kp = np.stack([results[b]["kpT"].T.reshape(SEQ, 8, 128) for b in range(B)])[None]
    vp = np.stack([results[b]["vp"].reshape(SEQ, 8, 128) for b in range(B)])[None]
    cbp = np.stack([results[b]["cbpT"].T for b in range(B)])[None]
    ccp = np.stack([results[b]["ccpT"].T for b in range(B)])[None]
    hp = np.stack([results[b]["hpT"][:, 0] for b in range(B)])[None]
    ks = np.concatenate([results[c]["ksT"].T.reshape(16, 8, 8, 128) for c in range(NCORES)], axis=0)[None]
    vs = np.concatenate([results[c]["vs"].reshape(16, 8, 8, 128) for c in range(NCORES)], axis=0)[None]
    cbs = np.concatenate([results[c]["cbsT"].transpose(1, 2, 0) for c in range(NCORES)], axis=0)[None]
    ccs = np.concatenate([results[c]["ccsT"].transpose(1, 2, 0) for c in range(NCORES)], axis=0)[None]
    hs = np.concatenate([results[c]["hsT"].T for c in range(NCORES)], axis=0)[None]
    outs = (yp, ys, kp, vp, cbp, ccp, hp, ks, vs, cbs, ccs, hs)
    return tuple(np.ascontiguousarray(o, dtype=np.float32) for o in outs)


def kernel(**inputs):
    nc = _get_nc()
    maps = prep_inputs(inputs)
    res = run_bass_kernel_spmd(nc, maps, core_ids=list(range(NCORES)))
    return assemble(res.results)
```
